# Optimizing a Trainium2 kernel written in Bass

```python
import numpy as np
import jax, jax.numpy as jnp
from jax import lax

D_MODEL = 1024
BATCH = 8
SEQ = 2048
DEPTH = 1

HEAD_DIM = 64
NSA_HEADS = 8
NSA_KV = 2
NSA_REP = NSA_HEADS // NSA_KV
RWKV_HEADS = 8
NSA_WIDTH = NSA_HEADS * HEAD_DIM
RWKV_WIDTH = RWKV_HEADS * HEAD_DIM
MIX_WIDTH = NSA_WIDTH + RWKV_WIDTH
KV_WIDTH = NSA_KV * HEAD_DIM
ROPE_DIM = HEAD_DIM // 4
ROPE_THETA = 500000.0
CMP_BLOCK = 32
CMP_STRIDE = 16
CMP_HIDDEN = 256
SEL_BLOCK = 64
SEL_TOPK = 8
WINDOW = 512
Q_BLOCK = 128
DECAY_RANK = 64
ICLR_RANK = 64
RWKV_SHIFT_WIDTH = 3 * RWKV_WIDTH + DECAY_RANK + ICLR_RANK
IN_SIZES = (NSA_WIDTH, KV_WIDTH, KV_WIDTH, KV_WIDTH, KV_WIDTH, KV_WIDTH, KV_WIDTH,
            3 * NSA_HEADS, NSA_WIDTH, RWKV_SHIFT_WIDTH, RWKV_WIDTH)
IN_WIDTH = sum(IN_SIZES)
SCALE = HEAD_DIM ** -0.5
RMS_EPS = 1e-6
GN_EPS = 64e-5
NEG_INF = -1e30
FORCE_BONUS = 1e3

kernel_name = "hymba_nsa_rwkv7_hybrid"


def _split(h, sizes):
    idx = np.cumsum(sizes)[:-1].tolist()
    return jnp.split(h, idx, axis=-1)


def rms_norm(x, g):
    xf = x.astype(jnp.float32)
    y = xf * lax.rsqrt(jnp.mean(xf * xf, axis=-1, keepdims=True) + RMS_EPS)
    return (y * g.astype(jnp.float32)).astype(x.dtype)


def partial_rope(t, pos):
    half = ROPE_DIM // 2
    inv = ROPE_THETA ** (-jnp.arange(half, dtype=jnp.float32) / half)
    ang = pos.astype(jnp.float32)[:, None] * inv[None, :]
    cos, sin = jnp.cos(ang), jnp.sin(ang)
    tr = t[..., :ROPE_DIM].astype(jnp.float32)
    t1, t2 = tr[..., :half], tr[..., half:]
    rot = jnp.concatenate([t1 * cos - t2 * sin, t2 * cos + t1 * sin], axis=-1).astype(t.dtype)
    return jnp.concatenate([rot, t[..., ROPE_DIM:]], axis=-1)


def masked_softmax(s, mask):
    s = jnp.where(mask, s.astype(jnp.float32), NEG_INF)
    p = jax.nn.softmax(s, axis=-1)
    return jnp.where(mask, p, 0.0)


def compress_blocks(kv, pos_emb, w1, w2):
    B, G, S, D = kv.shape
    ch = kv.reshape(B, G, S // CMP_STRIDE, CMP_STRIDE, D)
    blk = jnp.concatenate([ch[:, :, :-1], ch[:, :, 1:]], axis=3) + pos_emb
    flat = blk.reshape(B, G, -1, CMP_BLOCK * D)
    return jax.nn.silu(flat @ w1) @ w2


def cmp_to_sel_matrix(n_cmp, n_sel):
    c0 = np.arange(n_cmp)[:, None] * CMP_STRIDE
    s0 = np.arange(n_sel)[None, :] * SEL_BLOCK
    ov = np.clip(np.minimum(c0 + CMP_BLOCK, s0 + SEL_BLOCK) - np.maximum(c0, s0), 0, None)
    return jnp.asarray(ov / CMP_BLOCK, dtype=jnp.float32)


def nsa_mixer(q, kc, vc, ks, vs, kw, vw, gate_logits,
              pos_k, w1_k, w2_k, pos_v, w1_v, w2_v):
    B, S, _ = q.shape
    n_cmp = S // CMP_STRIDE - 1
    n_sel = S // SEL_BLOCK
    top_n = min(SEL_TOPK, n_sel)
    n_qb = S // Q_BLOCK
    pos = jnp.arange(S)
    f32 = jnp.float32
    qh = q.reshape(B, S, NSA_KV, NSA_REP, HEAD_DIM).transpose(0, 2, 3, 1, 4)

    def kvh(t):
        return t.reshape(B, S, NSA_KV, HEAD_DIM).transpose(0, 2, 1, 3)

    k_cmp = compress_blocks(kvh(kc), pos_k, w1_k, w2_k)
    v_cmp = compress_blocks(kvh(vc), pos_v, w1_v, w2_v)
    s_cmp = jnp.einsum('bgrsd,bgcd->bgrsc', qh, k_cmp) * SCALE
    cmp_end = jnp.arange(n_cmp) * CMP_STRIDE + CMP_BLOCK - 1
    p_cmp = masked_softmax(s_cmp, cmp_end[None, :] <= pos[:, None])
    o_cmp = jnp.einsum('bgrsc,bgcd->bgrsd', p_cmp, v_cmp.astype(f32))

    imp = jnp.einsum('bgrsc,cj->bgsj', p_cmp, cmp_to_sel_matrix(n_cmp, n_sel))
    t_blk = (pos // SEL_BLOCK)[:, None]
    j = jnp.arange(n_sel)[None, :]
    forced = ((j == 0) | (j == t_blk) | (j == t_blk - 1)).astype(f32)
    imp = jnp.where(j <= t_blk, imp + FORCE_BONUS * forced, -1.0)
    _, sel_idx = lax.top_k(imp, top_n)

    q_rot = partial_rope(qh, pos)
    ks_blk = partial_rope(kvh(ks), pos).reshape(B, NSA_KV, n_sel, SEL_BLOCK, HEAD_DIM)
    vs_blk = kvh(vs).reshape(B, NSA_KV, n_sel, SEL_BLOCK, HEAD_DIM)
    pad = ((0, 0), (0, 0), (WINDOW, 0), (0, 0))
    kw_pad = jnp.pad(partial_rope(kvh(kw), pos), pad)
    vw_pad = jnp.pad(kvh(vw), pad)
    bi = jnp.arange(B)[:, None, None, None]
    gi = jnp.arange(NSA_KV)[None, :, None, None]
    n_tok = top_n * SEL_BLOCK

    def block_fn(qb):
        q0 = qb * Q_BLOCK
        tq = q0 + jnp.arange(Q_BLOCK)
        qc = lax.dynamic_slice_in_dim(q_rot, q0, Q_BLOCK, axis=3)
        idx = lax.dynamic_slice_in_dim(sel_idx, q0, Q_BLOCK, axis=2)
        k_g = ks_blk[bi, gi, idx].reshape(B, NSA_KV, Q_BLOCK, n_tok, HEAD_DIM)
        v_g = vs_blk[bi, gi, idx].reshape(B, NSA_KV, Q_BLOCK, n_tok, HEAD_DIM)
        k_pos = (idx[..., None] * SEL_BLOCK + jnp.arange(SEL_BLOCK)).reshape(B, NSA_KV, Q_BLOCK, n_tok)
        s_sel = jnp.einsum('bgrqd,bgqmd->bgrqm', qc, k_g) * SCALE
        p_sel = masked_softmax(s_sel, (k_pos <= tq[:, None])[:, :, None])
        o_sel = jnp.einsum('bgrqm,bgqmd->bgrqd', p_sel, v_g.astype(f32))
        kwc = lax.dynamic_slice_in_dim(kw_pad, q0, WINDOW + Q_BLOCK, axis=2)
        vwc = lax.dynamic_slice_in_dim(vw_pad, q0, WINDOW + Q_BLOCK, axis=2)
        w_pos = q0 - WINDOW + jnp.arange(WINDOW + Q_BLOCK)
        diff = tq[:, None] - w_pos[None, :]
        w_mask = (diff >= 0) & (diff < WINDOW) & (w_pos[None, :] >= 0)
        s_win = jnp.einsum('bgrqd,bgkd->bgrqk', qc, kwc) * SCALE
        p_win = masked_softmax(s_win, w_mask)
        o_win = jnp.einsum('bgrqk,bgkd->bgrqd', p_win, vwc.astype(f32))
        return o_sel, o_win

    o_sel, o_win = lax.map(block_fn, jnp.arange(n_qb))

    def unblock(o):
        return o.transpose(1, 2, 3, 0, 4, 5).reshape(B, NSA_KV, NSA_REP, S, HEAD_DIM)

    g = jax.nn.sigmoid(gate_logits.astype(f32)).reshape(B, S, NSA_KV, NSA_REP, 3).transpose(0, 2, 3, 1, 4)
    o = g[..., 0:1] * o_cmp + g[..., 1:2] * unblock(o_sel) + g[..., 2:3] * unblock(o_win)
    return o.transpose(0, 3, 1, 2, 4).reshape(B, S, NSA_WIDTH)


def rwkv7_step(state, inp):
    r_t, w_t, k_t, v_t, kk_t, a_t = inp
    sa = jnp.einsum('bhvk,bhk->bhv', state, -kk_t)
    state = (state * w_t[:, :, None, :] + sa[..., None] * (kk_t * a_t)[:, :, None, :]
             + v_t[..., None] * k_t[:, :, None, :])
    y = jnp.einsum('bhvk,bhk->bhv', state, r_t)
    return state, y


def rwkv7_mixer(p, shift_mu, w0, w_up, a0, a_up, k_k, k_a, r_k, gn_w, gn_b):
    B, S, _ = p.shape
    f32 = jnp.float32
    prev = jnp.pad(p, ((0, 0), (1, 0), (0, 0)))[:, :-1]
    p = p + shift_mu * (prev - p)
    r, k, v, wd, ad = _split(p, (RWKV_WIDTH, RWKV_WIDTH, RWKV_WIDTH, DECAY_RANK, ICLR_RANK))
    w = (w0 + jnp.tanh(wd) @ w_up).astype(f32)
    decay = jnp.exp(-jnp.exp(-jax.nn.softplus(-w) - 0.5))
    a = jax.nn.sigmoid((a0 + ad @ a_up).astype(f32))

    def heads(t):
        return t.astype(f32).reshape(B, S, RWKV_HEADS, HEAD_DIM)

    kk = heads(k * k_k)
    kk = kk / jnp.maximum(jnp.sqrt(jnp.sum(kk * kk, axis=-1, keepdims=True)), 1e-12)
    k = k.astype(f32) * (1.0 + (a - 1.0) * k_a)
    rh, kh, vh, wh, ah = heads(r), heads(k), heads(v), heads(decay), heads(a)

    def tm(t):
        return t.transpose(1, 0, 2, 3)

    s0 = jnp.zeros((B, RWKV_HEADS, HEAD_DIM, HEAD_DIM), f32)
    _, y = lax.scan(rwkv7_step, s0, (tm(rh), tm(wh), tm(kh), tm(vh), tm(kk), tm(ah)))
    y = y.transpose(1, 0, 2, 3)
    mean = jnp.mean(y, axis=-1, keepdims=True)
    var = jnp.mean(jnp.square(y - mean), axis=-1, keepdims=True)
    y = (y - mean) * lax.rsqrt(var + GN_EPS)
    y = y * gn_w.reshape(RWKV_HEADS, HEAD_DIM) + gn_b.reshape(RWKV_HEADS, HEAD_DIM)
    y = y + jnp.sum(rh * kh * r_k.reshape(RWKV_HEADS, HEAD_DIM), axis=-1, keepdims=True) * vh
    return y.reshape(B, S, RWKV_WIDTH)


def hybrid_layer(x, norm_g, w_in, cmp_pos_k, cmp_w1_k, cmp_w2_k, cmp_pos_v, cmp_w1_v, cmp_w2_v,
                 shift_mu, decay_w0, decay_up, iclr_a0, iclr_up, k_k, k_a, r_k, gn_w, gn_b, w_out):
    h = rms_norm(x, norm_g)
    proj = h @ w_in
    (q, kc, vc, ks, vs, kw, vw, gl, g_nsa, rwkv_in, g_rwkv) = _split(proj, IN_SIZES)
    o_nsa = nsa_mixer(q, kc, vc, ks, vs, kw, vw, gl,
                      cmp_pos_k, cmp_w1_k, cmp_w2_k, cmp_pos_v, cmp_w1_v, cmp_w2_v)
    o_nsa = o_nsa * jax.nn.silu(g_nsa.astype(jnp.float32))
    o_rwkv = rwkv7_mixer(rwkv_in, shift_mu, decay_w0, decay_up, iclr_a0, iclr_up,
                         k_k, k_a, r_k, gn_w, gn_b)
    o_rwkv = o_rwkv * jax.nn.silu(g_rwkv.astype(jnp.float32))
    mix = jnp.concatenate([o_nsa, o_rwkv], axis=-1).astype(x.dtype)
    return x + mix @ w_out


def setup_inputs(seed: int = 0) -> dict:
    key = jax.random.key(seed)
    ks = jax.random.split(key, 24)
    f32 = jnp.float32

    def nrm(k, shape, scale):
        return jax.random.normal(k, shape, f32) * scale

    L = DEPTH
    return {
        "x": nrm(ks[0], (BATCH, SEQ, D_MODEL), 1.0),
        "norm_g": 1.0 + nrm(ks[1], (L, D_MODEL), 0.05),
        "w_in": nrm(ks[2], (L, D_MODEL, IN_WIDTH), D_MODEL ** -0.5),
        "cmp_pos_k": nrm(ks[3], (L, CMP_BLOCK, HEAD_DIM), 0.1),
        "cmp_w1_k": nrm(ks[4], (L, CMP_BLOCK * HEAD_DIM, CMP_HIDDEN), (CMP_BLOCK * HEAD_DIM) ** -0.5),
        "cmp_w2_k": nrm(ks[5], (L, CMP_HIDDEN, HEAD_DIM), CMP_HIDDEN ** -0.5),
        "cmp_pos_v": nrm(ks[6], (L, CMP_BLOCK, HEAD_DIM), 0.1),
        "cmp_w1_v": nrm(ks[7], (L, CMP_BLOCK * HEAD_DIM, CMP_HIDDEN), (CMP_BLOCK * HEAD_DIM) ** -0.5),
        "cmp_w2_v": nrm(ks[8], (L, CMP_HIDDEN, HEAD_DIM), CMP_HIDDEN ** -0.5),
        "shift_mu": jax.random.uniform(ks[9], (L, RWKV_SHIFT_WIDTH), f32),
        "decay_w0": jax.random.uniform(ks[10], (L, RWKV_WIDTH), f32, -4.0, 1.0),
        "decay_up": nrm(ks[11], (L, DECAY_RANK, RWKV_WIDTH), 0.1 * DECAY_RANK ** -0.5),
        "iclr_a0": nrm(ks[12], (L, RWKV_WIDTH), 0.5),
        "iclr_up": nrm(ks[13], (L, ICLR_RANK, RWKV_WIDTH), 0.5 * ICLR_RANK ** -0.5),
        "k_k": 0.85 + nrm(ks[14], (L, RWKV_WIDTH), 0.05),
        "k_a": 1.0 + nrm(ks[15], (L, RWKV_WIDTH), 0.05),
        "r_k": nrm(ks[16], (L, RWKV_WIDTH), 0.1),
        "gn_w": 1.0 + nrm(ks[17], (L, RWKV_WIDTH), 0.05),
        "gn_b": nrm(ks[18], (L, RWKV_WIDTH), 0.02),
        "w_out": nrm(ks[19], (L, MIX_WIDTH, D_MODEL), MIX_WIDTH ** -0.5),
        "final_g": 1.0 + nrm(ks[20], (D_MODEL,), 0.05),
    }


def reference(x, norm_g, w_in, cmp_pos_k, cmp_w1_k, cmp_w2_k, cmp_pos_v, cmp_w1_v, cmp_w2_v,
              shift_mu, decay_w0, decay_up, iclr_a0, iclr_up, k_k, k_a, r_k, gn_w, gn_b,
              w_out, final_g):
    h = x
    for l in range(DEPTH):
        h = hybrid_layer(h, norm_g[l], w_in[l], cmp_pos_k[l], cmp_w1_k[l], cmp_w2_k[l],
                         cmp_pos_v[l], cmp_w1_v[l], cmp_w2_v[l], shift_mu[l], decay_w0[l],
                         decay_up[l], iclr_a0[l], iclr_up[l], k_k[l], k_a[l], r_k[l],
                         gn_w[l], gn_b[l], w_out[l])
    return rms_norm(h, final_g)
```

```python
from contextlib import ExitStack
import os
STG = int(os.environ.get('KSTG', '99'))
NDUM = int(os.environ.get('KDUM', '0'))
KLAT = float(os.environ.get('KLAT', '0.6'))
KPE = float(os.environ.get('KPE', '1.0'))
KAD = float(os.environ.get('KAD', '0.85'))
DUMN = int(os.environ.get('KDUMN', '256'))
NDUM3 = int(os.environ.get('KDUM3', '0'))
NDUM4 = int(os.environ.get('KDUM4', '0'))

import numpy as np
import ml_dtypes

import concourse.bass as bass
import concourse.mybir as mybir
from concourse.bass_utils import run_bass_kernel_spmd

F32 = mybir.dt.float32
BF16 = mybir.dt.bfloat16
AF = mybir.ActivationFunctionType
ALU = mybir.AluOpType
AX = mybir.AxisListType
NPBF = ml_dtypes.bfloat16

T = 2048
D = 1024
NT = 16
P = 128
NEG = -1.0e5
BIGSEL = 1.0e5


class Sched:
    BLOCK = {"pe": "tensor", "act": "scalar", "dve": "vector", "pool": "gpsimd", "sp": "sync"}

    def __init__(self):
        self.ops = []

    def add(self, eng, fn, r=(), w=(), dma=False):
        psr = tuple(k for k in r if isinstance(k, tuple) and k and k[0] == "PS")
        r = tuple(r)
        w = tuple(w) + tuple(k for k in psr if k not in w)
        self.ops.append((eng, fn, tuple(r), tuple(w), dma))
        return len(self.ops) - 1

    def pe(self, fn, r=(), w=()):
        return self.add("pe", fn, r, w)

    def act(self, fn, r=(), w=()):
        return self.add("act", fn, r, w)

    def dve(self, fn, r=(), w=()):
        return self.add("dve", fn, r, w)

    def pool(self, fn, r=(), w=()):
        return self.add("pool", fn, r, w)

    def dma(self, fn, r=(), w=(), q="sp"):
        return self.add(q, fn, r, w, dma=True)

    def emit(self, nc, esems, dsems, swsems=()):
        ops = self.ops
        ndma = [i for i in range(len(ops)) if ops[i][4]]
        ops.append(("sp", None, (), (), False))
        n = len(ops)
        last_w = {}
        readers = {}
        deps = [dict() for _ in range(n)]
        for i, (eng, fn, rd, wr, dma) in enumerate(ops):
            for k in rd:
                j = last_w.get(k)
                if j is not None and j != i:
                    deps[i][j] = True
            for k in wr:
                j = last_w.get(k)
                if j is not None and j != i:
                    deps[i].setdefault(j, False)
                lastr = {}
                for r in readers.get(k, ()):
                    if r == i:
                        continue
                    if ops[r][4]:
                        deps[i].setdefault(r, False)
                    else:
                        lastr[ops[r][0]] = r
                for r in lastr.values():
                    deps[i].setdefault(r, False)
            for k in rd:
                readers.setdefault(k, []).append(i)
            for k in wr:
                last_w[k] = i
                readers[k] = []
        kept = [[] for _ in range(n)]
        signal = [bool(op[4]) for op in ops]
        for i in range(n):
            ei, _, _, _, di = ops[i]
            for j, raw in deps[i].items():
                ej, _, _, _, dj = ops[j]
                if not dj and not di and ej == ei:
                    if ei == "pe":
                        continue
                kept[i].append(j)
                signal[j] = True
        kept[n - 1] = list(ndma)
        ecount = {e: 0 for e in self.BLOCK}
        dcount = [0] * len(dsems)
        dnext = 0
        swnext = 0
        sig = [None] * n
        for i in range(n):
            if not signal[i]:
                continue
            e, _, _, _, dma = ops[i]
            if dma and e == "pool":
                sig[i] = (swsems[swnext], 16, 16)
                swnext += 1
            elif dma:
                s = dnext % len(dsems)
                dnext += 1
                dcount[s] += 16
                sig[i] = (dsems[s], dcount[s], 16)
            else:
                ecount[e] += 1
                sig[i] = (esems[e], ecount[e], 1)
        waited = {e: {} for e in self.BLOCK}
        waits = [[] for _ in range(n)]
        for i in range(n):
            e = ops[i][0]
            need = {}
            for j in kept[i]:
                sem, val, _ = sig[j]
                key = id(sem)
                if val > need.get(key, (None, 0))[1]:
                    need[key] = (sem, val)
            if ops[i][4] and e != "pool" and sig[i][1] > 16:
                sem, val, _ = sig[i]
                if val - 16 > need.get(id(sem), (None, 0))[1]:
                    need[id(sem)] = (sem, val - 16)
            for key, (sem, val) in need.items():
                if waited[e].get(key, 0) >= val:
                    continue
                waited[e][key] = val
                waits[i].append((sem, val))
        with nc.Block() as block:
            for e, bname in self.BLOCK.items():
                idx = [i for i in range(n) if ops[i][0] == e]
                if not idx:
                    continue

                def body(engine, idx=idx):
                    for i in idx:
                        for sem, val in waits[i]:
                            engine.wait_ge(sem, val)
                        fn = ops[i][1]
                        if fn is None:
                            continue
                        ins = fn(engine)
                        if signal[i]:
                            ins.then_inc(sig[i][0], sig[i][2])

                getattr(block, bname)(body)


def _fd(ap):
    n = 1
    for d in ap.shape[1:]:
        n *= int(d)
    return n


def _c(fn, cost):
    fn.cost = cost
    return fn


def I_mm(out, lhsT, rhs, start=True, stop=True):
    return _c(lambda e: e.matmul(out, lhsT=lhsT, rhs=rhs, start=start, stop=stop, skip_group_check=True),
              0.08 + max(_fd(out), 64) / 2400.0 * (4 if rhs.dtype == F32 else 1))


def I_tr(out, in_, ident):
    return _c(lambda e: e.transpose(out=out, in_=in_, identity=ident), 0.1)


def I_act(out, in_, func, **kw):
    return _c(lambda e: e.activation(out=out, in_=in_, func=func, **kw), 0.2 + _fd(out) / 1200.0)


def I_tt(out, in0, in1, op):
    return _c(lambda e: e.tensor_tensor(out=out, in0=in0, in1=in1, op=op), 0.1 + _fd(out) / 960.0)


def I_ts(out, in0, s1, s2, op0, op1=None):
    if op1 is None:
        return _c(lambda e: e.tensor_scalar(out=out, in0=in0, scalar1=s1, scalar2=None, op0=op0), 0.1 + _fd(out) / 960.0)
    return _c(lambda e: e.tensor_scalar(out=out, in0=in0, scalar1=s1, scalar2=s2, op0=op0, op1=op1), 0.1 + _fd(out) / 960.0)


def I_stt(out, in0, scalar, in1, op0, op1):
    return _c(lambda e: e.scalar_tensor_tensor(out=out, in0=in0, scalar=scalar, in1=in1, op0=op0, op1=op1), 0.1 + _fd(out) / 960.0)


def I_cp(out, in_):
    return _c(lambda e: e.tensor_copy(out=out, in_=in_), 0.1 + _fd(out) / 960.0)


def I_rcp(out, in_):
    return _c(lambda e: e.reciprocal(out=out, in_=in_), 0.1 + _fd(out) / 960.0)


def I_ms(out, val):
    return _c(lambda e: e.memset(out, val), 0.1 + _fd(out) / 960.0)


def I_dma(out, in_):
    return _c(lambda e: e.dma_start(out=out, in_=in_), 2.0)


_COST = {"pe": 0.12, "act": 0.65, "dve": 0.7, "pool": 1.0, "sp": 2.0}


def merge2(A, B, offset_b=0.0):
    seqs = [A, B]
    idx = [0, 0]
    eng_free = {}
    kw = [{}, {}]
    kr = [{}, {}]
    out = []

    def ready(si):
        eng, fn, r, w, dma = seqs[si][idx[si]]
        t = max(eng_free.get(eng, 0.0), offset_b if si == 1 else 0.0)
        for k in r:
            tw, ew = kw[si].get(k, (0.0, eng))
            t = max(t, tw + (KLAT if ew != eng else 0.05))
        for k in w:
            tw, ew = kw[si].get(k, (0.0, eng))
            t = max(t, tw + (KLAT if ew != eng else 0.05))
            tr_, er = kr[si].get(k, (0.0, eng))
            t = max(t, tr_ + (KLAT if er != eng else 0.0))
        return t

    while idx[0] < len(A) or idx[1] < len(B):
        cands = [si for si in (0, 1) if idx[si] < len(seqs[si])]
        if len(cands) == 2:
            t0, t1 = ready(0), ready(1)
            if t0 < t1 or (t0 == t1 and idx[0] * len(B) <= idx[1] * len(A)):
                si = 0
            else:
                si = 1
        else:
            si = cands[0]
        op = seqs[si][idx[si]]
        t = ready(si)
        eng = op[0]
        cost = getattr(op[1], "cost", None)
        if cost is None:
            cost = _COST.get(eng, 0.5)
        if eng == "pool":
            cost = 0.2 + 1.6 * cost
        if eng == "pe":
            cost = cost * KPE
        elif eng in ("act", "dve"):
            cost = cost * KAD
        te = t + cost
        eng_free[eng] = te
        for k in op[3]:
            kw[si][k] = (te, eng)
        for k in op[2]:
            if te > kr[si].get(k, (0.0, eng))[0]:
                kr[si][k] = (te, eng)
        out.append(op)
        idx[si] += 1
    merge2.last_makespan = max(eng_free.values()) if eng_free else 0.0
    return out


def bc(ap, shape, axis):
    return ap.unsqueeze(axis).broadcast_to(shape)


def host_consts():
    c = {}
    c["ident"] = np.eye(128, dtype=np.float32).astype(NPBF)
    c["identf"] = np.eye(128, dtype=np.float32)
    pos = np.arange(T, dtype=np.float32)
    half = 8
    inv = (500000.0 ** (-np.arange(half, dtype=np.float32) / half)).astype(np.float32)
    ang = pos[:, None] * inv[None, :]
    cos = np.cos(ang).astype(np.float32).reshape(NT, 128, half).transpose(1, 0, 2)
    sin = np.sin(ang).astype(np.float32).reshape(NT, 128, half).transpose(1, 0, 2)
    c["ropec"] = np.ascontiguousarray(cos)
    c["ropes"] = np.ascontiguousarray(sin)
    E = (np.arange(T)[None, :] // 64 == np.arange(32)[:, None]).astype(np.float32)
    c["emat"] = E.astype(NPBF)
    c0 = np.arange(127)[:, None] * 16
    s0 = np.arange(32)[None, :] * 64
    ov = np.clip(np.minimum(c0 + 32, s0 + 64) - np.maximum(c0, s0), 0, None) / 32.0
    c["mcs"] = ov.astype(np.float32).astype(NPBF)
    cend = np.arange(127)[:, None] * 16 + 31
    c["cmpb"] = np.where(cend <= np.arange(T)[None, :], 0.0, NEG).astype(np.float32).astype(NPBF)
    kk = np.arange(128)[:, None]
    qq = np.arange(128)[None, :]
    c["cb"] = np.where(kk > qq, NEG, 0.0).astype(np.float32).astype(NPBF)
    c["acb"] = np.where(kk > qq, 0.0, NEG).astype(np.float32).astype(NPBF)
    t = np.arange(T)
    tb = (t // 64)[:, None]
    j = np.arange(32)[None, :]
    valid = (j <= tb).astype(np.float32)
    forced = ((j == 0) | (j == tb) | (j == tb - 1)).astype(np.float32)
    fv = 1000.0 * forced * valid + valid - 1.0
    c["valid"] = np.ascontiguousarray(valid.reshape(NT, 128, 32).transpose(1, 0, 2))
    c["fv"] = np.ascontiguousarray(fv.reshape(NT, 128, 32).transpose(1, 0, 2).astype(np.float32))
    rm = np.ones((128, 512), np.float32)
    rm[:, ::128] = 0.0
    c["rmask"] = rm.astype(NPBF)
    bd = np.zeros((128, 128), np.float32)
    bd[:64, :64] = 1.0
    bd[64:, 64:] = 1.0
    c["onesbd"] = bd
    ind = np.zeros((128, 2), np.float32)
    ind[:64, 0] = 1.0
    ind[64:, 1] = 1.0
    c["ind2"] = ind.astype(NPBF)
    pp_ = np.arange(128)[:, None]
    ff_ = np.arange(128)[None, :]
    c["m_su"] = (ff_ > pp_).astype(np.float32).astype(NPBF)
    c["m_ui"] = (ff_ >= pp_).astype(np.float32).astype(NPBF)
    c["m_sl"] = (ff_ < pp_).astype(np.float32).astype(NPBF)
    return c


IN_SIZES = (512, 128, 128, 128, 128, 128, 128, 24, 512, 1664, 512)
OFFS = np.concatenate([[0], np.cumsum(IN_SIZES)]).tolist()
(O_Q, O_KC, O_VC, O_KS, O_VS, O_KW, O_VW, O_GL, O_GN, O_RW, O_GR, O_END) = OFFS


def host_weights(inp):
    w = {}
    w_in = np.asarray(inp["w_in"][0], dtype=np.float32)

    def cols(*rngs):
        m = np.concatenate([w_in[:, a:b] for a, b in rngs], axis=1)
        return np.ascontiguousarray(m.reshape(8, 128, -1).transpose(1, 0, 2))

    w["w_rope"] = cols((O_Q, O_Q + 512), (O_KS, O_KS + 128), (O_KW, O_KW + 128))
    w["w_v"] = cols((O_VS, O_VS + 128), (O_VW, O_VW + 128))
    w["w_gn"] = cols((O_GN, O_GN + 512))
    w["w_gr"] = cols((O_GR, O_GR + 512))
    w["w_gl"] = cols((O_GL, O_GL + 24))
    w["w_kvc"] = cols((O_KC, O_KC + 128), (O_VC, O_VC + 128))
    w["w_rw"] = cols((O_RW, O_RW + 1664))
    w["gbc"] = np.ascontiguousarray(np.broadcast_to(np.asarray(inp["norm_g"][0], np.float32)[None, :], (128, D)))
    w["fgbc"] = np.ascontiguousarray(np.broadcast_to(np.asarray(inp["final_g"], np.float32)[None, :], (128, D)))
    for kind in ("k", "v"):
        w1 = np.asarray(inp["cmp_w1_" + kind][0], np.float32)
        w["w1" + kind] = np.ascontiguousarray(w1.reshape(32, 64, 256).transpose(1, 0, 2))
        w2 = np.asarray(inp["cmp_w2_" + kind][0], np.float32)
        w["w2" + kind] = np.ascontiguousarray(w2.reshape(2, 128, 64).transpose(1, 0, 2))
        pe = np.asarray(inp["cmp_pos_" + kind][0], np.float32)
        w["pos" + kind] = np.ascontiguousarray(pe.T)
    w["mu13"] = np.ascontiguousarray(np.asarray(inp["shift_mu"][0], np.float32).reshape(13, 128).T)
    for nm, key in (("w0c", "decay_w0"), ("a0c", "iclr_a0"), ("kkc", "k_k"), ("kac", "k_a"), ("rkc", "r_k")):
        w[nm] = np.ascontiguousarray(np.asarray(inp[key][0], np.float32).reshape(4, 128).T)
    w["wup"] = np.ascontiguousarray(np.concatenate([np.asarray(inp["decay_up"][0], np.float32),
                                                    np.asarray(inp["iclr_up"][0], np.float32)], axis=0))
    w["gnwbc"] = np.ascontiguousarray(np.broadcast_to(np.asarray(inp["gn_w"][0], np.float32)[None, :], (128, 512)))
    w["gnbbc"] = np.ascontiguousarray(np.broadcast_to(np.asarray(inp["gn_b"][0], np.float32)[None, :], (128, 512)))
    w_out = np.asarray(inp["w_out"][0], np.float32)
    w["w_out"] = np.ascontiguousarray(w_out.reshape(8, 128, D).transpose(1, 0, 2))
    return w


def build(debug=(), upto=99):
    nc = bass.Bass("TRN2", target_bir_lowering=False)
    dr = {}

    def din(name, shape, dt=F32):
        dr[name] = nc.dram_tensor(name, list(shape), dt, kind="ExternalInput").ap()
        return dr[name]

    def dout(name, shape, dt=F32):
        dr[name] = nc.dram_tensor(name, list(shape), dt, kind="ExternalOutput").ap()
        return dr[name]

    x = din("x", [T, D])
    out = dout("out", [T, D])
    for nm, shp in (("w_rope", [128, 8, 768]), ("w_v", [128, 8, 256]), ("w_gn", [128, 8, 512]),
                    ("w_gr", [128, 8, 512]), ("w_gl", [128, 8, 24]), ("w_kvc", [128, 8, 256]),
                    ("w_rw", [128, 8, 1664]), ("gbc", [128, D]), ("fgbc", [128, D]),
                    ("w1k", [64, 32, 256]), ("w1v", [64, 32, 256]), ("w2k", [128, 2, 64]),
                    ("w2v", [128, 2, 64]), ("posk", [64, 32]), ("posv", [64, 32]),
                    ("w_out", [128, 8, D]), ("ropec", [128, NT, 8]), ("ropes", [128, NT, 8]),
                    ("valid", [128, NT, 32]), ("fv", [128, NT, 32]), ("identf", [128, 128]),
                    ("mu13", [128, 13]), ("w0c", [128, 4]), ("a0c", [128, 4]), ("kkc", [128, 4]), ("kac", [128, 4]),
                    ("rkc", [128, 4]), ("wup", [128, 512]), ("gnwbc", [128, 512]), ("gnbbc", [128, 512]),
                    ("onesbd", [128, 128])):
        din(nm, shp)
    for nm, shp in (("ident", [128, 128]), ("emat", [32, T]), ("mcs", [127, 32]), ("cmpb", [127, T]),
                    ("cb", [128, 128]), ("acb", [128, 128]), ("rmask", [128, 512]), ("ind2", [128, 2]),
                    ("m_su", [128, 128]), ("m_ui", [128, 128]), ("m_sl", [128, 128])):
        din(nm, shp, BF16)

    dbg_out = {}

    outer = ExitStack()
    with outer:
        def mk(es):
            def sb(name, shape, dt):
                return es.enter_context(nc.sbuf_tensor(name, list(shape), dt))

            def ps(name, shape, dt):
                return es.enter_context(nc.psum_tensor(name, list(shape), dt))

            def sems(tag, nsw=0, nd=8):
                esems = {e: outer.enter_context(nc.semaphore(tag + "_" + e)) for e in Sched.BLOCK}
                dsems = [outer.enter_context(nc.semaphore("%s_d%d" % (tag, i))) for i in range(nd)]
                swsems = [outer.enter_context(nc.semaphore("%s_w%d" % (tag, i))) for i in range(nsw)]
                return esems, dsems, swsems
            return sb, ps, sems

        sb0, ps0, sems0 = mk(outer)
        xT = sb0("xT", [128, 8, T], BF16)
        mix = sb0("mix", [128, NT, D], BF16)
        idt = sb0("idt", [128, 128], BF16)

        def dump(S, name, tens_ap, shape, dt, rkeys):
            if name not in debug:
                return
            d = dout("dbg_" + name, shape, dt)
            dbg_out[name] = d
            S.dma(lambda e: e.dma_start(out=d, in_=tens_ap), r=rkeys, w=["dbg_" + name])
            S.add("sp", None, r=["dbg_" + name])

        nsa = ExitStack()
        with nsa:
            sb1, ps1, sems1 = mk(nsa)
            QG = sb1("QG", [96, 2, 4, T], BF16)
            QU = sb1("QU", [64, 2, 4, T], BF16)
            KE = sb1("KE", [96, 2, T], BF16)
            KW = sb1("KW", [64, 2, T], BF16)
            VP = sb1("VP", [128, NT, 4, 65], BF16)
            GL = sb1("GL", [128, NT, 24], F32)
            KcT = sb1("KcT", [64, 2, 128], BF16)
            VcM = sb1("VcM", [128, 2, 97], BF16)

            p1 = ExitStack()
            with p1:
                sb, ps, sems = mk(p1)
                S = Sched()
                esems, dsems, swsems = sems("p1", 5)
                xs = sb("xs", [128, 2, D], F32)
                xb = sb("xb", [128, 2, D], BF16)
                gbc = sb("gbc_s", [128, D], F32)
                ss = sb("ss", [128, NT], F32)
                lnt = sb("lnt", [128, NT], F32)
                rstd = sb("rstd", [128, NT], F32)
                wbuf = sb("wbuf", [128, 8 * 768 + 8 * 512], BF16)
                qku = sb("qku", [128, 2, 512], BF16)
                qkk = sb("qkk", [128, 2, 768], F32)
                qkr = sb("qkr", [128, 2, 768], BF16)
                ra = sb("ra", [128, 2, 96], F32)
                rb = sb("rb", [128, 2, 96], F32)
                rc = sb("rc", [128, 2, 96], F32)
                rd_ = sb("rd", [128, 2, 96], F32)
                ropec = sb("ropec_s", [128, NT, 8], F32)
                ropes = sb("ropes_s", [128, NT, 8], F32)
                pt = ps("pt", [128, 2, 8, 128], BF16)
                pp = ps("pp", [128, 4, 512], F32)
                ptq = ps("ptq", [64, 2, 8, 128], BF16)

                S.pool(lambda e: e.memset(VP[:, :, :, 64:65], 1.0), w=["VPone"])
                S.pool(lambda e: e.memset(ss[:], 0.0), w=["ss"])

                wgroups = [("w_rope", 768), ("w_v", 256), ("w_gn", 512), ("w_gr", 512), ("w_gl", 24)]
                wslot = {}

                def wview(slot, ncols):
                    o0 = 0 if slot == 0 else 8 * 768
                    return wbuf[:, o0:o0 + 8 * ncols].rearrange("p (k n) -> p k n", k=8)

                def load_w(gi, after=()):
                    nm, ncols = wgroups[gi]
                    slot = gi % 2
                    wslot[nm] = slot
                    S.dma(lambda e: e.dma_start(out=wview(slot, ncols), in_=dr[nm]), r=list(after), w=[("wbuf", slot)], q="pool")

                ppc = [0, 0]

                def proj_tok2(S_, tt, wnm, ncols, c0, c1):
                    b_ = tt % 2
                    bk = 2 * b_ + ppc[b_] % 2
                    ppc[b_] += 1
                    wv = wview(wslot[wnm], ncols)
                    for k in range(8):
                        S_.pe(I_mm(pp[:, bk, 0:c1 - c0], xT[:, k, tt * 128:(tt + 1) * 128], wv[:, k, c0:c1], k == 0, k == 7),
                              r=[("xT", tt), ("wbuf", wslot[wnm])], w=[("PS", "pp", bk)])
                    return bk

                def tile_thread(tt, S_):
                    b = tt % 2
                    tsl = slice(tt * 128, (tt + 1) * 128)
                    if tt >= 2:
                        S_.dma(I_dma(xs[:, b, :], x[tt * 128:(tt + 1) * 128, :]), w=[("xs", b)])
                    S_.act(I_act(xb[:, b, :], xs[:, b, :], AF.Square, accum_out=ss[:, tt:tt + 1]),
                           r=[("xs", b), "ss"], w=[("xb", b), ("ss", tt)])
                    S_.act(I_act(lnt[:, tt:tt + 1], ss[:, tt:tt + 1], AF.Ln, scale=1.0 / D, bias=1e-6), r=[("ss", tt)], w=[("lnt", tt)])
                    S_.act(I_act(rstd[:, tt:tt + 1], lnt[:, tt:tt + 1], AF.Exp, scale=-0.5), r=[("lnt", tt)], w=[("rstd", tt)])
                    S_.dve(I_stt(xb[:, b, :], xs[:, b, :], rstd[:, tt:tt + 1], gbc[:], ALU.mult, ALU.mult),
                           r=[("xs", b), ("rstd", tt), "gbc"], w=[("xb", b)])
                    for k in range(8):
                        S_.pe(I_tr(pt[:, b, k, :], xb[:, b, k * 128:(k + 1) * 128], idt[:]), r=[("xb", b), "idt"], w=[("PS", "pt", b)])
                    S_.act(I_act(xT[:, :, tsl], pt[:, b, :, :], AF.Copy), r=[("PS", "pt", b)], w=[("xT", tt)])
                    if STG < 2:
                        return
                    bk0 = proj_tok2(S_, tt, "w_rope", 768, 0, 512)
                    bk1 = proj_tok2(S_, tt, "w_rope", 768, 512, 768)
                    S_.act(I_act(qkk[:, b, 0:512], pp[:, bk0, :], AF.Copy), r=[("PS", "pp", bk0)], w=[("qkk", b)])
                    S_.dve(I_cp(qkk[:, b, 512:768], pp[:, bk1, 0:256]), r=[("PS", "pp", bk1)], w=[("qkk", b)])
                    q3 = qkk[:, b, :].rearrange("p (h d) -> p h d", d=64)
                    o3 = qkr[:, b, :].rearrange("p (h d) -> p h d", d=64)
                    cosb = bc(ropec[:, tt, :], [128, 12, 8], 1)
                    sinb = bc(ropes[:, tt, :], [128, 12, 8], 1)
                    ra3 = ra[:, b, :].rearrange("p (h d) -> p h d", d=8)
                    rb3 = rb[:, b, :].rearrange("p (h d) -> p h d", d=8)
                    rc3 = rc[:, b, :].rearrange("p (h d) -> p h d", d=8)
                    rd3 = rd_[:, b, :].rearrange("p (h d) -> p h d", d=8)
                    S_.dve(I_tt(ra3, q3[:, :, 0:8], cosb, ALU.mult), r=[("qkk", b), "ropec"], w=[("ra", b)])
                    S_.dve(I_tt(rb3, q3[:, :, 8:16], sinb, ALU.mult), r=[("qkk", b), "ropes"], w=[("rb", b)])
                    S_.dve(I_tt(rc3, q3[:, :, 8:16], cosb, ALU.mult), r=[("qkk", b), "ropec"], w=[("rc", b)])
                    S_.dve(I_tt(rd3, q3[:, :, 0:8], sinb, ALU.mult), r=[("qkk", b), "ropes"], w=[("rd", b)])
                    S_.dve(I_tt(o3[:, :, 0:8], ra3, rb3, ALU.subtract), r=[("ra", b), ("rb", b)], w=[("qkr", b)])
                    S_.dve(I_tt(o3[:, :, 8:16], rc3, rd3, ALU.add), r=[("rc", b), ("rd", b)], w=[("qkr", b)])
                    S_.dve(I_cp(o3[:, :, 16:64], q3[:, :, 16:64]), r=[("qkk", b)], w=[("qkr", b)])
                    S_.act(I_act(qku[:, b, :], qkk[:, b, 0:512], AF.Copy), r=[("qkk", b)], w=[("qku", b)])
                    kq = ("PS", "ptq", b)
                    for h in range(8):
                        S_.pe(I_tr(ptq[:, b, h, :], qkr[:, b, h * 64:(h + 1) * 64], idt[:]), r=[("qkr", b), "idt"], w=[kq])
                    S_.act(I_act(QG[0:64, 0, :, tsl], ptq[:, b, 0:4, :], AF.Copy), r=[kq], w=[("QGq", 0, tt)])
                    S_.dve(I_cp(QG[0:64, 1, :, tsl], ptq[:, b, 4:8, :]), r=[kq], w=[("QGq", 1, tt)])
                    for h in range(4):
                        S_.pe(I_tr(ptq[:, b, h, :], qkr[:, b, (8 + h) * 64:(9 + h) * 64], idt[:]), r=[("qkr", b), "idt"], w=[kq])
                    S_.act(I_act(KE[0:64, :, tsl], ptq[:, b, 0:2, :], AF.Copy), r=[kq], w=[("KEk", tt)])
                    S_.dve(I_cp(KW[0:64, :, tsl], ptq[:, b, 2:4, :]), r=[kq], w=[("KWk", tt)])
                    for h in range(8):
                        S_.pe(I_tr(ptq[:, b, h, :], qku[:, b, h * 64:(h + 1) * 64], idt[:]), r=[("qku", b), "idt"], w=[kq])
                    S_.act(I_act(QU[:, 0, :, tsl], ptq[:, b, 0:4, :], AF.Copy), r=[kq], w=[("QU", 0, tt)])
                    S_.dve(I_cp(QU[:, 1, :, tsl], ptq[:, b, 4:8, :]), r=[kq], w=[("QU", 1, tt)])
                    if STG < 3:
                        return
                    bk = proj_tok2(S_, tt, "w_v", 256, 0, 256)
                    S_.act(I_act(VP[:, tt, :, 0:64], pp[:, bk, 0:256].rearrange("p (a d) -> p a d", d=64), AF.Copy),
                           r=[("PS", "pp", bk)], w=[("VP", tt)])

                S.dma(I_dma(xs[:, 0, :], x[0:128, :]), w=[("xs", 0)])
                S.dma(I_dma(idt[:], dr["ident"]), w=["idt"])
                S.dma(I_dma(gbc[:], dr["gbc"]), w=["gbc"])
                S.dma(I_dma(xs[:, 1, :], x[128:256, :]), w=[("xs", 1)])
                S.dma(I_dma(ropec[:], dr["ropec"]), w=["ropec"])
                S.dma(I_dma(ropes[:], dr["ropes"]), w=["ropes"])
                for g in range(2):
                    S.dma(I_dma(KE[64:96, g, :], dr["emat"]), w=[("KEm", g)])
                load_w(0, after=[("xs", 1), "gbc"])
                load_w(1, after=[("xs", 1), "gbc"])
                seqs = [[], []]
                for tt in range(NT):
                    Sp = Sched()
                    tile_thread(tt, Sp)
                    seqs[tt % 2].extend(Sp.ops)
                S.ops.extend(merge2(seqs[0], seqs[1]))
                load_w(2)
                load_w(3)
                ppn = [0]

                def proj_tok(tt, wnm, ncols, c0, c1):
                    bk = ppn[0] % 4
                    ppn[0] += 1
                    wv = wview(wslot[wnm], ncols)
                    for k in range(8):
                        S.pe(I_mm(pp[:, bk, 0:c1 - c0], xT[:, k, tt * 128:(tt + 1) * 128], wv[:, k, c0:c1], k == 0, k == 7),
                             r=[("xT", tt), ("wbuf", wslot[wnm])], w=[("PS", "pp", bk)])
                    return bk

                for tt in range(NT if STG >= 5 else 0):
                    bk = proj_tok(tt, "w_gn", 512, 0, 512)
                    S.act(lambda e, tt=tt, bk=bk: e.activation(out=mix[:, tt, 0:512], in_=pp[:, bk, :], func=AF.Silu),
                          r=[("PS", "pp", bk)], w=[("mixn", tt)])
                load_w(4)
                for tt in range(NT if STG >= 6 else 0):
                    bk = proj_tok(tt, "w_gr", 512, 0, 512)
                    S.act(lambda e, tt=tt, bk=bk: e.activation(out=mix[:, tt, 512:1024], in_=pp[:, bk, :], func=AF.Silu),
                          r=[("PS", "pp", bk)], w=[("mixr", tt)])
                for tt in range(NT if STG >= 7 else 0):
                    bk = proj_tok(tt, "w_gl", 24, 0, 24)
                    S.act(lambda e, tt=tt, bk=bk: e.activation(out=GL[:, tt, :], in_=pp[:, bk, 0:24], func=AF.Sigmoid),
                          r=[("PS", "pp", bk)], w=[("GL", tt)])

                allx = [("xT", t_) for t_ in range(NT)]
                dump(S, "xT", xT[:], [128, 8, T], BF16, allx)
                dump(S, "QG", QG[0:64], [64, 2, 4, T], BF16, [("QGq", g, t_) for g in range(2) for t_ in range(NT)])
                dump(S, "KE", KE[:], [96, 2, T], BF16, [("KEk", t_) for t_ in range(NT)] + [("KEm", 0), ("KEm", 1)])
                dump(S, "KW", KW[:], [64, 2, T], BF16, [("KWk", t_) for t_ in range(NT)])
                dump(S, "VP", VP[:], [128, NT, 4, 65], BF16, [("VP", t_) for t_ in range(NT)] + ["VPone"])
                dump(S, "GL", GL[:], [128, NT, 24], F32, [("GL", t_) for t_ in range(NT)])
                dump(S, "mix1", mix[:], [128, NT, D], BF16, [("mixn", t_) for t_ in range(NT)] + [("mixr", t_) for t_ in range(NT)])
                S.emit(nc, esems, dsems, swsems)
                print("phase1 ops", len(S.ops))

            if upto <= 1:
                return nc, dbg_out

            pc = ExitStack()
            with pc:
                sb, ps, sems = mk(pc)
                S = Sched()
                esems, dsems, swsems = sems("pc", 13)
                w1b = sb("w1b", [128, 2, 16, 256], BF16)
                kvT = sb("kvT", [128, 2, 16, 128], BF16)
                wkv = sb("wkv", [128, 8, 256], BF16)
                pp2 = ps("pp2", [128, 2, 512], F32)
                S.dma(lambda e: e.dma_start(out=wkv[:], in_=dr["w_kvc"]), w=["wkv"], q="pool")
                for ki in range(2):
                    for tb in range(4):
                        bk = (ki * 4 + tb) % 2
                        for k in range(8):
                            S.pe(I_mm(pp2[:, bk, :], wkv[:, k, ki * 128:(ki + 1) * 128], xT[:, k, tb * 512:(tb + 1) * 512], k == 0, k == 7),
                                 r=["wkv"], w=[("PS", "pp2", bk)])
                        dst = kvT[:, ki, :, tb * 32:(tb + 1) * 32].rearrange("p i c -> p c i")
                        src = pp2[:, bk, :].rearrange("p (c i) -> p c i", i=16)
                        if tb % 2 == 0:
                            S.act(I_act(dst, src, AF.Copy), r=[("PS", "pp2", bk)], w=[("kvT", ki)])
                        else:
                            S.dve(I_cp(dst, src), r=[("PS", "pp2", bk)], w=[("kvT", ki)])
                w2b = sb("w2b", [128, 2, 2, 64], BF16)
                posb = sb("posb", [64, 2, 32], BF16)
                hT = sb("hT", [128, 4, 2, 128], BF16)
                bias1 = sb("bias1", [128, 4], F32)
                pc1 = ps("pc1", [128, 8, 128], F32)
                pb = ps("pb", [128, 512], F32)
                pk_ = ps("pk", [128, 512], F32)
                pv_ = ps("pv", [128, 512], F32)
                pk = pk_[0:64, 0:256].rearrange("p (g c) -> p g c", c=128)
                pv = pv_[:, 0:128].rearrange("p (g c) -> p g c", c=64)
                for ki, kind in enumerate(("k", "v")):
                    S.dma(lambda e, ki=ki, kind=kind: e.dma_start(out=w2b[:, ki, :, :], in_=dr["w2" + kind]), w=[("w2b", ki)], q="pool")
                    S.dma(lambda e, ki=ki, kind=kind: e.dma_start(out=posb[:, ki, :], in_=dr["pos" + kind]), w=[("posb", ki)], q="pool")
                S.pool(lambda e: e.memset(VcM[:, :, 64:65], 1.0), w=["VcMone"])
                for g in range(2):
                    S.dma(lambda e, g=g: e.dma_start(out=VcM[0:127, g, 65:97], in_=dr["mcs"]), w=[("VcMm", g)])
                first_in_bank = {}
                for ki, kind in enumerate(("k", "v")):
                    for hb in range(2):
                        for half in range(2):
                            S.dma(I_dma(w1b[half * 64:(half + 1) * 64, hb, :, :], dr["w1" + kind][:, hb * 16:(hb + 1) * 16, :]),
                                  w=[("w1b", hb, half)], q="pool")
                        for il in range(16):
                            i = hb * 16 + il
                            for mc in range(2):
                                for g in range(2):
                                    gs = slice(g * 64, (g + 1) * 64)
                                    idx = g * 4 + ki * 2 + mc
                                    bank = ("PS", "pc1", g)
                                    st = (bank, ki) not in first_in_bank
                                    first_in_bank[(bank, ki)] = 1
                                    src = kvT[gs, ki, i, 0:127] if i < 16 else kvT[gs, ki, i - 16, 1:128]
                                    S.pe(I_mm(pc1[:, idx, 0:127], w1b[gs, hb, il, mc * 128:(mc + 1) * 128], src, st, i == 31),
                                         r=[("w1b", hb, g), ("kvT", ki)], w=[bank])
                                bank = ("PS", "pb")
                                st = bank not in first_in_bank
                                first_in_bank[bank] = 1
                                S.pe(I_mm(pb[:, ki * 2 + mc:ki * 2 + mc + 1], w1b[0:64, hb, il, mc * 128:(mc + 1) * 128], posb[:, ki, i:i + 1], st, i == 31),
                                     r=[("w1b", hb, 0), ("posb", ki)], w=[bank])
                S.dve(lambda e: e.tensor_copy(out=bias1[:], in_=pb[:, 0:4]), r=[("PS", "pb")], w=["bias1"])
                for j in range(4):
                    for mc in range(2):
                        ki = j // 2
                        g_ = j % 2
                        S.act(I_act(hT[:, j, mc, 0:127], pc1[:, g_ * 4 + ki * 2 + mc, 0:127], AF.Silu, bias=bias1[:, ki * 2 + mc:ki * 2 + mc + 1]),
                              r=[("PS", "pc1", g_), "bias1"], w=[("hT", j)])
                for g in range(2):
                    for mc in range(2):
                        S.pe(lambda e, g=g, mc=mc: e.matmul(pk[:, g, 0:127], lhsT=w2b[:, 0, mc, :], rhs=hT[:, g, mc, 0:127],
                                                            start=(mc == 0), stop=(mc == 1)),
                             r=[("w2b", 0), ("hT", g)], w=[("PS", "pk")])
                    S.act(lambda e, g=g: e.activation(out=KcT[:, g, 0:127], in_=pk[:, g, 0:127], func=AF.Copy),
                          r=[("PS", "pk")], w=[("KcT", g)])
                for g in range(2):
                    for mc in range(2):
                        S.pe(lambda e, g=g, mc=mc: e.matmul(pv[0:127, g, :], lhsT=hT[:, 2 + g, mc, 0:127], rhs=w2b[:, 1, mc, :],
                                                            start=(mc == 0), stop=(mc == 1)),
                             r=[("w2b", 1), ("hT", 2 + g)], w=[("PS", "pv")])
                    S.dve(lambda e, g=g: e.tensor_copy(out=VcM[0:127, g, 0:64], in_=pv[0:127, g, :]),
                          r=[("PS", "pv")], w=[("VcMv", g)])
                dump(S, "KcT", KcT[:, :, 0:127], [64, 2, 127], BF16, [("KcT", 0), ("KcT", 1)])
                dump(S, "VcM", VcM[0:127], [127, 2, 97], BF16, [("VcMv", 0), ("VcMv", 1), ("VcMm", 0), ("VcMm", 1), "VcMone"])
                S.emit(nc, esems, dsems, swsems)
                print("phase1c ops", len(S.ops))
            if upto <= 2:
                return nc, dbg_out

            p2 = ExitStack()
            with p2:
                sb, ps, sems = mk(p2)
                S = Sched()
                esems, dsems, swsems = sems("p2", 0)
                cmpb = sb("cmpb_s", [128, T], BF16)
                cb = sb("cb_s", [128, 128], BF16)
                acb = sb("acb_s", [128, 128], BF16)
                valid = sb("valid_s", [128, NT, 32], F32)
                fv = sb("fv_s", [128, NT, 32], F32)
                PT = sb("PT", [128, 4, 512], BF16)
                ocg = sb("ocg", [128, 2, 256], F32)
                selpad = sb("selpad", [128, 2, 96], BF16)
                dn = sb("dn", [128, 2, 4], F32)
                rden = sb("rden", [128, 2, 4], F32)
                coefc = sb("coefc", [128, 2, 4], F32)
                impa = sb("impa", [128, 2, 32], F32)
                imp2 = sb("imp2", [128, 2, 32], F32)
                m8 = sb("m8", [128, 2, 8], F32)
                dsw = sb("dsw", [128, 2, 8], F32)
                rsw = sb("rsw", [128, 2, 8], F32)
                csw = sb("csw", [128, 2, 8], F32)
                t1 = sb("t1", [128, 2, 256], F32)
                t2 = sb("t2", [128, 2, 256], F32)
                sc = ps("sc", [128, 4, 512], F32)
                junk = ps("junkp", [128, 512], F32)
                oc_ = ps("oc", [128, 512], F32)
                osel_ = ps("osel", [128, 512], F32)
                owin_ = ps("owin", [128, 512], F32)
                oc = oc_[:, 0:388].rearrange("p (r c) -> p r c", c=97)
                selT_ = oc_[:, 400:464].bitcast(BF16)
                selT = selT_[0:96, :]
                osel = osel_[:, 0:260].rearrange("p (r c) -> p r c", c=65)
                owin = owin_[:, 0:260].rearrange("p (r c) -> p r c", c=65)

                S.dma(lambda e: e.dma_start(out=cmpb[0:127, :], in_=dr["cmpb"]), w=["cmpb"])
                S.dma(lambda e: e.dma_start(out=cb[:], in_=dr["cb"]), w=["cb"])
                S.dma(lambda e: e.dma_start(out=acb[:], in_=dr["acb"]), w=["acb"])
                S.dma(lambda e: e.dma_start(out=valid[:], in_=dr["valid"]), w=["valid"])
                S.dma(lambda e: e.dma_start(out=fv[:], in_=dr["fv"]), w=["fv"])
                S.pool(lambda e: e.memset(selpad[:], 0.0), w=[("selpad", 0), ("selpad", 1)])

                iters = [(g, qt) for qt in range(NT) for g in range(2)]
                if STG < 20:
                    iters = iters[:STG]
                cnt = {"sc": 0, "pt": 0}

                def GLv(qt, g, br):
                    return GL[:, qt, :].rearrange("p (h b) -> p h b", b=3)[:, g * 4:(g + 1) * 4, br]

                def qsl(qt):
                    return slice(qt * 128, (qt + 1) * 128)

                def A1(n):
                    g, qt = iters[n]
                    bk = cnt["sc"] % 4
                    cnt["sc"] += 1
                    pbf = cnt["pt"] % 4
                    cnt["pt"] += 1
                    S.pe(lambda e: e.matmul(sc[0:127, bk, :], lhsT=KcT[:, g, 0:127], rhs=QU[:, g, :, qsl(qt)],
                                            start=True, stop=False),
                         r=[("KcT", g), ("QU", g, qt)], w=[("PS", "sc", bk)])
                    S.pe(lambda e: e.matmul(sc[0:127, bk, :], lhsT=idt[0:127, 0:127],
                                            rhs=bc(cmpb[0:127, qsl(qt)], [127, 4, 128], 1), start=False, stop=True),
                         r=["idt", "cmpb"], w=[("PS", "sc", bk)])
                    S.act(lambda e: e.activation(out=PT[0:127, pbf, :], in_=sc[0:127, bk, :], func=AF.Exp, scale=0.125),
                          r=[("PS", "sc", bk)], w=[("PT", pbf)])
                    return pbf

                def A3(n, pbf):
                    g, qt = iters[n]
                    ib = n % 2
                    for r in range(4):
                        S.pe(lambda e, r=r: e.matmul(oc[:, r, :], lhsT=PT[0:127, pbf, r * 128:(r + 1) * 128], rhs=VcM[0:127, g, :],
                                                     start=True, stop=True, skip_group_check=True),
                             r=[("PT", pbf), ("VcMv", g), ("VcMm", g), "VcMone"], w=[("PS", "oc")])
                    S.dve(lambda e: e.tensor_scalar(out=dn[:, ib, :], in0=oc[:, :, 64], scalar1=1e-30, scalar2=None, op0=ALU.max),
                          r=[("PS", "oc")], w=[("dn", ib)])
                    S.dve(lambda e: e.reciprocal(out=rden[:, ib, :], in_=dn[:, ib, :]), r=[("dn", ib)], w=[("rden", ib)])
                    S.dve(lambda e: e.tensor_tensor(out=coefc[:, ib, :], in0=rden[:, ib, :], in1=GLv(qt, g, 0), op=ALU.mult),
                          r=[("rden", ib), ("GL", qt)], w=[("coefc", ib)])
                    S.dve(lambda e: e.tensor_tensor(out=ocg[:, ib, :].rearrange("p (r d) -> p r d", d=64), in0=oc[:, :, 0:64],
                                                    in1=bc(coefc[:, ib, :], [128, 4, 64], 2), op=ALU.mult),
                          r=[("PS", "oc"), ("coefc", ib)], w=[("ocg", ib)])
                    S.dve(lambda e: e.tensor_scalar(out=impa[:, ib, :], in0=oc[:, 0, 65:97], scalar1=rden[:, ib, 0:1], scalar2=None,
                                                    op0=ALU.mult),
                          r=[("PS", "oc"), ("rden", ib)], w=[("impa", ib)])
                    for r in range(1, 4):
                        S.dve(lambda e, r=r: e.scalar_tensor_tensor(out=impa[:, ib, :], in0=oc[:, r, 65:97], scalar=rden[:, ib, r:r + 1],
                                                                    in1=impa[:, ib, :], op0=ALU.mult, op1=ALU.add),
                              r=[("PS", "oc"), ("rden", ib), ("impa", ib)], w=[("impa", ib)])
                    S.dve(lambda e: e.tensor_tensor(out=imp2[:, ib, :], in0=impa[:, ib, :], in1=valid[:, qt, :], op=ALU.mult),
                           r=[("impa", ib), "valid"], w=[("imp2", ib)])
                    S.dve(lambda e: e.tensor_tensor(out=imp2[:, ib, :], in0=imp2[:, ib, :], in1=fv[:, qt, :], op=ALU.add),
                           r=[("imp2", ib), "fv"], w=[("imp2", ib)])
                    S.dve(lambda e: e.max(out=m8[:, ib, :], in_=imp2[:, ib, :]), r=[("imp2", ib)], w=[("m8", ib)])
                    S.dve(lambda e: e.tensor_scalar(out=selpad[:, ib, 64:96], in0=imp2[:, ib, :], scalar1=m8[:, ib, 7:8], scalar2=1.0,
                                                    op0=ALU.is_ge, op1=ALU.subtract),
                          r=[("imp2", ib), ("m8", ib)], w=[("selpad", ib)])

                def A5(n):
                    g, qt = iters[n]
                    ib = n % 2
                    S.pe(lambda e: e.transpose(out=selT, in_=selpad[:, ib, :], identity=idt[:]),
                         r=[("selpad", ib), "idt"], w=[("PS", "oc")])
                    S.act(lambda e: e.activation(out=QG[64:96, g, :, qsl(qt)], in_=bc(selT_[64:96, :], [32, 4, 128], 1),
                                                 func=AF.Copy, scale=BIGSEL),
                          r=[("PS", "oc")], w=[("QGm", g, qt)])

                def tiles_of(n):
                    g, qt = iters[n]
                    L = []
                    for kt in range(0, qt + 1):
                        L.append(("s", kt))
                    for kt in range(max(0, qt - 4), qt + 1):
                        L.append(("w", kt))
                    return L

                def QK(n, tl):
                    g, qt = iters[n]
                    br, kt = tl
                    bk = cnt["sc"] % 4
                    cnt["sc"] += 1
                    ksl = slice(kt * 128, (kt + 1) * 128)
                    if br == "s":
                        bias = cb if kt == qt else None
                        S.pe(lambda e: e.matmul(sc[:, bk, :], lhsT=KE[0:96, g, ksl], rhs=QG[0:96, g, :, qsl(qt)],
                                                start=True, stop=(bias is None)),
                             r=[("KEk", kt), ("KEm", g), ("QGq", g, qt), ("QGm", g, qt)], w=[("PS", "sc", bk)])
                    else:
                        bias = cb if kt == qt else (acb if kt == qt - 4 else None)
                        S.pe(lambda e: e.matmul(sc[:, bk, :], lhsT=KW[0:64, g, ksl], rhs=QG[0:64, g, :, qsl(qt)],
                                                start=True, stop=(bias is None)),
                             r=[("KWk", kt), ("QGq", g, qt)], w=[("PS", "sc", bk)])
                    if bias is not None:
                        S.pe(lambda e: e.matmul(sc[:, bk, :], lhsT=idt[:], rhs=bc(bias[:], [128, 4, 128], 1), start=False, stop=True),
                             r=["idt", "cb", "acb"], w=[("PS", "sc", bk)])
                    return bk

                def EXPPV(n, tl, bk, first_s, first_w, last):
                    g, qt = iters[n]
                    br, kt = tl
                    pbf = cnt["pt"] % 4
                    cnt["pt"] += 1
                    S.act(lambda e: e.activation(out=PT[:, pbf, :], in_=sc[:, bk, :], func=AF.Exp, scale=0.125),
                          r=[("PS", "sc", bk)], w=[("PT", pbf)])
                    acc = osel if br == "s" else owin
                    akey = ("PS", "osel") if br == "s" else ("PS", "owin")
                    vi = g if br == "s" else 2 + g
                    first = first_s if br == "s" else first_w
                    for r in range(4):
                        S.pe(lambda e, r=r: e.matmul(acc[:, r, :], lhsT=PT[:, pbf, r * 128:(r + 1) * 128], rhs=VP[:, kt, vi, :],
                                                     start=(first and r == 0), stop=last, skip_group_check=True),
                             r=[("PT", pbf), ("VP", kt), "VPone"], w=[akey])
                    for _ in range(NDUM):
                        S.pe(I_mm(junk[:, 0:DUMN], idt[:], PT[:, pbf, 0:DUMN], True, True), r=[("PT", pbf)], w=[])

                def FIN(n):
                    g, qt = iters[n]
                    ib = n % 2
                    S.dve(lambda e: e.tensor_copy(out=dsw[:, ib, 0:4], in_=osel[:, :, 64]), r=[("PS", "osel")], w=[("dsw", ib)])
                    S.dve(lambda e: e.tensor_copy(out=dsw[:, ib, 4:8], in_=owin[:, :, 64]), r=[("PS", "owin")], w=[("dsw", ib)])
                    S.dve(lambda e: e.reciprocal(out=rsw[:, ib, :], in_=dsw[:, ib, :]), r=[("dsw", ib)], w=[("rsw", ib)])
                    S.dve(lambda e: e.tensor_tensor(out=csw[:, ib, 0:4], in0=rsw[:, ib, 0:4], in1=GLv(qt, g, 1), op=ALU.mult),
                          r=[("rsw", ib), ("GL", qt)], w=[("csw", ib)])
                    S.dve(lambda e: e.tensor_tensor(out=csw[:, ib, 4:8], in0=rsw[:, ib, 4:8], in1=GLv(qt, g, 2), op=ALU.mult),
                          r=[("rsw", ib), ("GL", qt)], w=[("csw", ib)])
                    t1v = t1[:, ib, :].rearrange("p (r d) -> p r d", d=64)
                    t2v = t2[:, ib, :].rearrange("p (r d) -> p r d", d=64)
                    S.dve(lambda e: e.tensor_tensor(out=t1v, in0=osel[:, :, 0:64], in1=bc(csw[:, ib, 0:4], [128, 4, 64], 2), op=ALU.mult),
                          r=[("PS", "osel"), ("csw", ib)], w=[("t1", ib)])
                    S.dve(lambda e: e.tensor_tensor(out=t2v, in0=owin[:, :, 0:64], in1=bc(csw[:, ib, 4:8], [128, 4, 64], 2), op=ALU.mult),
                          r=[("PS", "owin"), ("csw", ib)], w=[("t2", ib)])
                    S.dve(lambda e: e.tensor_tensor(out=t1[:, ib, :], in0=t1[:, ib, :], in1=ocg[:, ib, :], op=ALU.add),
                           r=[("t1", ib), ("ocg", ib)], w=[("t1", ib)])
                    S.dve(lambda e: e.tensor_tensor(out=t1[:, ib, :], in0=t1[:, ib, :], in1=t2[:, ib, :], op=ALU.add),
                           r=[("t1", ib), ("t2", ib)], w=[("t1", ib)])
                    msl = mix[:, qt, g * 256:(g + 1) * 256]
                    S.dve(lambda e: e.tensor_tensor(out=msl, in0=t1[:, ib, :], in1=msl, op=ALU.mult),
                           r=[("t1", ib), ("mixn", qt)], w=[("mixo", g, qt)])

                if iters:
                    pbf0 = A1(0)
                    A3(0, pbf0)
                    A5(0)
                for n in range(len(iters)):
                    L = tiles_of(n)
                    nxt = n + 1 < len(iters)
                    if nxt:
                        pbn = A1(n + 1)
                    bks = {}
                    LA = int(os.environ.get('KLA', '3'))
                    for ti in range(min(LA, len(L))):
                        bks[ti] = QK(n, L[ti])
                    seen = set()
                    for ti in range(len(L)):
                        if ti + LA < len(L):
                            bks[ti + LA] = QK(n, L[ti + LA])
                        if nxt and ti == min(1, len(L) - 1):
                            A3(n + 1, pbn)
                        br, kt = L[ti]
                        first = br not in seen
                        seen.add(br)
                        qt = iters[n][1]
                        last = (kt == qt)
                        EXPPV(n, L[ti], bks[ti], first, first, last)
                    FIN(n)
                    if nxt:
                        A5(n + 1)
                mo = [("mixo", g, qt) for (g, qt) in iters]
                dump(S, "mix2", mix[:], [128, NT, D], BF16, mo)
                S.emit(nc, esems, dsems, swsems)
                print("phase2 ops", len(S.ops))
            if upto <= 3:
                return nc, dbg_out

        p3 = ExitStack()
        with p3:
            sb, ps, sems = mk(p3)
            S = Sched()
            esems, dsems, swsems = sems("p3", 3)
            wrw = sb("wrw", [128, 8, 1664], BF16)
            wupb = sb("wupb", [128, 512], BF16)
            mu = sb("mu_s", [128, 13], F32)
            mum = sb("mum_s", [128, 13], F32)
            prm = sb("prm", [128, 8, 4], F32)
            gnw = sb("gnw_s", [128, 512], F32)
            gnb = sb("gnb_s", [128, 512], F32)
            rmask = sb("rmask_s", [128, 512], BF16)
            onesbd = sb("onesbd_s", [128, 128], F32)
            ind2 = sb("ind2_s", [128, 2], BF16)
            msu = sb("msu", [128, 128], BF16)
            mui = sb("mui", [128, 128], BF16)
            msl = sb("msl", [128, 128], BF16)
            CAR = sb("CAR", [128, 5, 3], F32)
            TA = sb("TA", [128, 4, 512], BF16)
            PW = sb("PW", [128, 520], F32)
            DW = sb("DW", [128, 512], F32)
            P3 = sb("P3", [128, 2, 3, 520], F32)
            D3 = sb("D3", [128, 2, 2, 512], F32)
            XS = sb("XS", [128, 2, 2, 512], F32)
            E4 = sb("E4", [128, 2, 4, 512], BF16)
            OP = sb("OP", [128, 2, 8, 512], BF16)
            TM = sb("TM", [128, 2, 4, 4, 128], BF16)
            RK = sb("RK", [128, 2, 4, 2], F32)
            GC = sb("GC", [128, 2, 4], F32)
            PA = sb("PA", [128, 2, 2, 2, 4, 128], BF16)
            PTA = sb("PTA", [128, 2, 2, 2, 4, 128], BF16)
            XX = sb("XX", [128, 2, 2, 4, 128], BF16)
            AT = sb("AT", [128, 2, 2, 4, 4, 128], BF16)
            Hf = sb("Hf", [128, 4, 64], F32)
            Hb = sb("Hb", [128, 4, 64], BF16)
            U0b = sb("U0b", [128, 2, 2, 128], BF16)
            Ub = sb("Ub", [128, 2, 2, 128], BF16)
            st1 = sb("st1", [128, 2, 8], F32)
            st2 = sb("st2", [128, 2, 8], F32)
            st3 = sb("st3", [128, 2, 8], F32)
            pb3 = ps("pb3", [128, 8, 512], F32)
            S.dma(I_dma(wrw[:, :, 1536:1664], dr["w_rw"][:, :, 1536:1664]), w=["wrw_wa"], q="pool")
            S.dma(I_dma(wrw[:, :, 0:1536], dr["w_rw"][:, :, 0:1536]), w=["wrw"], q="pool")
            S.dma(I_dma(wupb[:], dr["wup"]), w=["wupb"], q="pool")
            S.dma(I_dma(mu[:], dr["mu13"]), w=["mu"])
            S.dve(I_ts(mum[:], mu[:], -1.0, 1.0, ALU.mult, ALU.add), r=["mu"], w=["mum"])
            for i_, nm in enumerate(("w0c", "a0c", "kkc", "kac", "rkc")):
                S.dma(I_dma(prm[:, i_, :], dr[nm]), w=["prm"])
            S.dve(I_ts(prm[:, 5:7, :], prm[:, 0:2, :], -1.0, None, ALU.mult), r=["prm"], w=["prm2"])
            S.dve(I_ts(prm[:, 7, :], prm[:, 3, :], -1.0, 1.0, ALU.mult, ALU.add), r=["prm"], w=["prm3"])
            for nm, dst in (("gnwbc", gnw), ("gnbbc", gnb), ("rmask", rmask), ("onesbd", onesbd), ("ind2", ind2),
                            ("m_su", msu), ("m_ui", mui), ("m_sl", msl)):
                S.dma(I_dma(dst[:], dr[nm]), w=[nm])
            S.pool(I_ms(CAR[:], 0.0), w=["CAR"])
            S.pool(I_ms(Hf[:], 0.0), w=[("Hf", j_) for j_ in range(4)])
            S.pool(I_ms(Hb[:], 0.0), w=[("Hb", j_) for j_ in range(4)])
            PRM = ["prm", "prm2", "prm3"]
            NTB = 4 if STG >= 40 else (min(4, max(1, STG - 29)) if STG >= 30 else 4)
            NJ = 4

            def blk_head(tb):
                tsl = slice(tb * 512, (tb + 1) * 512)
                tpar = tb
                for k in range(8):
                    S.pe(I_mm(pb3[:, 0, :], wrw[:, k, 1536:1664], xT[:, k, tsl], k == 0, k == 7), r=["wrw_wa"], w=[("PS", "b", 0)])
                S.act(I_act(PW[:, 8:520], pb3[:, 0, :], AF.Copy), r=[("PS", "b", 0)], w=["PW"])
                S.dve(I_cp(PW[:, 7:8], CAR[:, 4, 0:1]), r=["CAR", ("CARw", 4)], w=["PW"])
                S.dve(I_cp(CAR[:, 4, 0:1], PW[:, 519:520]), r=["PW"], w=[("CARw", 4)])
                S.dve(I_tt(DW[:], PW[:, 7:519], PW[:, 8:520], ALU.subtract), r=["PW"], w=["DW"])
                S.dve(I_stt(DW[:], DW[:], mu[:, 12:13], PW[:, 8:520], ALU.mult, ALU.add), r=["DW", "PW", "mu"], w=["DW"])
                S.act(I_act(PW[0:64, 8:520], DW[0:64, :], AF.Exp, scale=2.0), r=["DW"], w=["PW"])
                S.dve(I_ts(PW[0:64, 8:520], PW[0:64, 8:520], 1.0, None, ALU.add), r=["PW"], w=["PW"])
                S.dve(I_rcp(PW[0:64, 8:520], PW[0:64, 8:520]), r=["PW"], w=["PW"])
                S.dve(I_ts(TA[0:64, tpar, :], PW[0:64, 8:520], -2.0, 1.0, ALU.mult, ALU.add), r=["PW"], w=[("TA", tpar)])
                S.act(I_act(TA[64:128, tpar, :], DW[64:128, :], AF.Copy), r=["DW"], w=[("TA", tpar)])

            def pair(tb, j, S):
                tsl = slice(tb * 512, (tb + 1) * 512)
                tpar = tb
                par = (tb * 4 + j) % 2
                BG, BT, BN0, BN1 = 4 * par, 4 * par + 1, 4 * par + 2, 4 * par + 3
                ptv = pb3[:, BT, :].bitcast(BF16).rearrange("p (a k t) -> p a k t", a=2, k=4, t=128)
                kP = [("P3", par, i_) for i_ in range(3)]
                kD = [("D3", par, 0), ("D3", par, 1)]
                kX = [("XS", par, 0), ("XS", par, 1)]
                kOP = [("OP", par, i_) for i_ in range(8)]
                kE = [("E4", par, i_) for i_ in range(4)]
                for i_ in range(3):
                    c0 = i_ * 512 + j * 128
                    bk = (BG, BN0, BN1)[i_]
                    for k in range(8):
                        S.pe(I_mm(pb3[:, bk, :], wrw[:, k, c0:c0 + 128], xT[:, k, tsl], k == 0, k == 7), r=["wrw"], w=[("PS", "b", bk)])
                    if False:
                        pass
                    else:
                        S.act(I_act(P3[:, par, i_, 8:520], pb3[:, bk, :], AF.Copy), r=[("PS", "b", bk)], w=[kP[i_]])
                S.act(I_act(P3[:, par, :, 7:8], CAR[:, j, :].unsqueeze(2), AF.Copy), r=["CAR", ("CARw", j)], w=kP)
                S.act(I_act(CAR[:, j, :].unsqueeze(2), P3[:, par, :, 519:520], AF.Copy), r=kP, w=[("CARw", j)])

                def shift(i_, out_ap, okeys):
                    dd = D3[:, par, i_ % 2, :]
                    S.act(I_act(dd, P3[:, par, i_, 8:520], AF.Copy, scale=mum[:, i_ * 4 + j:i_ * 4 + j + 1]), r=[kP[i_], "mum"], w=[kD[i_ % 2]])
                    S.dve(I_stt(out_ap, P3[:, par, i_, 7:519], mu[:, i_ * 4 + j:i_ * 4 + j + 1], dd, ALU.mult, ALU.add),
                          r=[kD[i_ % 2], kP[i_], "mu"], w=okeys)

                shift(0, XS[:, par, 0, :], [kX[0]])
                shift(1, XS[:, par, 1, :], [kX[1]])
                shift(2, OP[:, par, 7, :], [kOP[7]])
                S0 = P3[:, par, 0, 8:520]
                S1 = P3[:, par, 1, 8:520]
                S2 = P3[:, par, 2, 8:520]
                S3 = D3[:, par, 0, :]
                S4 = D3[:, par, 1, :]
                XR = XS[:, par, 0, :]
                XK = XS[:, par, 1, :]
                k0, k1, k2, k3, k4 = kP[0], kP[1], kP[2], kD[0], kD[1]
                bkw = BT
                S.pe(I_mm(pb3[:, bkw, :], wupb[0:64, j * 128:(j + 1) * 128], TA[0:64, tpar, :]), r=["wupb", ("TA", tpar)], w=[("PS", "b", bkw)])
                S.act(I_act(S0, pb3[:, bkw, :], AF.Exp, scale=-1.0, bias=prm[:, 5, j:j + 1]), r=[("PS", "b", bkw)] + PRM + [kX[0]], w=[k0])
                S.act(I_act(S0, S0, AF.Ln, bias=1.0), r=[k0], w=[k0])
                S.act(I_act(S0, S0, AF.Exp, scale=-1.0, bias=-0.5), r=[k0], w=[k0])
                bka = BG
                S.pe(I_mm(pb3[:, bka, :], wupb[64:128, j * 128:(j + 1) * 128], TA[64:128, tpar, :]), r=["wupb", ("TA", tpar)], w=[("PS", "b", bka)])
                S.act(I_act(S1, pb3[:, bka, :], AF.Exp, scale=-1.0, bias=prm[:, 6, j:j + 1]), r=[("PS", "b", bka)] + PRM + [kX[1]], w=[k1])
                S.act(I_act(S1, S1, AF.Ln, bias=1.0), r=[k1], w=[k1])
                S.act(I_act(S1, S1, AF.Exp, scale=-1.0), r=[k1], w=[k1])
                S.act(I_act(S2, XK, AF.Copy, scale=prm[:, 2, j:j + 1]), r=[kX[1], kOP[7]] + PRM, w=[k2])
                S.act(I_act(S3, S2, AF.Square), r=[k2, kX[0], kX[1], kOP[7]], w=[k3])
                bkn = BN0
                S.pe(I_mm(pb3[:, bkn, :], onesbd[:], S3), r=["onesbd", k3], w=[("PS", "b", bkn)])
                S.act(I_act(S3, pb3[:, bkn, :], AF.Ln, bias=1e-24), r=[("PS", "b", bkn)], w=[k3])
                S.act(I_act(S3, S3, AF.Exp, scale=-0.5), r=[k3], w=[k3])
                S.dve(I_tt(S2, S2, S3, ALU.mult), r=[k2, k3], w=[k2])
                S.act(I_act(S3, S1, AF.Identity, scale=prm[:, 3, j:j + 1], bias=prm[:, 7, j:j + 1]), r=[k1, k2] + PRM, w=[k3])
                S.dve(I_tt(XK, XK, S3, ALU.mult), r=[kX[1], k3, k2], w=[kX[1]])
                S.dve(I_tt(S1, S2, S1, ALU.mult), r=[k2, k1, k3], w=[k1])
                S.dve(lambda e: e.tensor_tensor_scan(out=S3, data0=rmask[:], data1=S0, initial=0.0, op0=ALU.mult, op1=ALU.subtract),
                      r=["rmask", k0, kX[1], k1], w=[k3])
                S.dve(I_tt(S4, S3, S0, ALU.add), r=[k3, k0], w=[k4])
                S.act(I_act(E4[:, par, 0, :], S4, AF.Exp), r=[k4], w=[kE[0]])
                S.act(I_act(E4[:, par, 1, :], S3, AF.Exp), r=[k3], w=[kE[1]])
                S.act(I_act(E4[:, par, 2, :], S3, AF.Exp, scale=-1.0), r=[k3], w=[kE[2]])
                c3 = S3.rearrange("p (c t) -> p c t", t=128)
                S.act(I_act(GC[:, par, :], c3[:, :, 127], AF.Exp), r=[k3], w=[("GC", par)])
                S.dve(I_tt(S4.rearrange("p (c t) -> p c t", t=128), bc(c3[:, :, 127], [128, 4, 128], 2), c3, ALU.subtract), r=[k3, kE[0]], w=[k4])
                S.act(I_act(E4[:, par, 3, :], S4, AF.Exp), r=[k4], w=[kE[3]])
                S.dve(I_stt(OP[:, par, 0, :], S2, -1.0, E4[:, par, 0, :], ALU.mult, ALU.mult), r=[k2, kE[0]], w=[kOP[0]])
                S.dve(I_tt(OP[:, par, 1, :], XR, E4[:, par, 1, :], ALU.mult), r=[kX[0], kE[1]], w=[kOP[1]])
                S.dve(I_tt(OP[:, par, 2, :], S1, E4[:, par, 2, :], ALU.mult), r=[k1, kE[2]], w=[kOP[2]])
                S.dve(I_tt(OP[:, par, 3, :], XK, E4[:, par, 2, :], ALU.mult), r=[kX[1], kE[2]], w=[kOP[3]])
                S.dve(I_tt(OP[:, par, 4, :], S1, E4[:, par, 3, :], ALU.mult), r=[k1, kE[3]], w=[kOP[4]])
                S.dve(I_tt(OP[:, par, 5, :], XK, E4[:, par, 3, :], ALU.mult), r=[kX[1], kE[3]], w=[kOP[5]])
                S.dve(I_stt(OP[:, par, 6, :], XR, prm[:, 4, j:j + 1], XK, ALU.mult, ALU.mult), r=[kX[0], kX[1]] + PRM, w=[kOP[6]])
                for c in range(4):
                    csl = slice(c * 128, (c + 1) * 128)
                    hb_ = c % 2
                    for ki_, kind in enumerate((4, 5, 7)):
                        S.pe(I_tr(ptv[:, hb_, ki_, :], OP[:, par, kind, csl], idt[:]), r=[kOP[kind], "idt"], w=[("PS", "b", BT)])
                    S.act(I_act(TM[:, par, c, 0:3, :], ptv[:, hb_, 0:3, :], AF.Copy), r=[("PS", "b", BT)], w=[("TM", par, c)])
                    bkr = BG
                    S.pe(I_mm(pb3[:, bkr, 0:2], OP[:, par, 6, csl], ind2[:]), r=[kOP[6], "ind2"], w=[("PS", "b", bkr)])
                    S.act(I_act(RK[:, par, c, :], pb3[:, bkr, 0:2], AF.Copy), r=[("PS", "b", bkr)], w=[("RK", par)])
                S_pair = S
                heads = []
                for h in range(2):
                    S = Sched()
                    heads.append(S)
                    hs = slice(h * 64, (h + 1) * 64)
                    kAT = ("AT", par, h)
                    hbanks = ((BN0, BG), (BN1, BT))[h]
                    hcnt = [0]

                    def nxt(_pool, hbanks=hbanks, hcnt=hcnt):
                        b_ = hbanks[hcnt[0] % 2]
                        hcnt[0] += 1
                        return b_
                    bL = hbanks[1]
                    for c in range(4):
                        csl = slice(c * 128, (c + 1) * 128)
                        bA = hbanks[0]
                        gA = pb3[:, bA, :].rearrange("p (k t) -> p k t", t=128)
                        kA = ("PS", "b", bA)
                        S.pe(I_mm(gA[:, 0, :], OP[hs, par, 2, csl], OP[hs, par, 0, csl], True, True), r=[kOP[2], kOP[0]], w=[kA])
                        S.pe(I_mm(gA[:, 1, :], OP[hs, par, 3, csl], OP[hs, par, 0, csl], False, True), r=[kOP[3], kOP[0]], w=[kA])
                        S.pe(I_mm(gA[:, 2, :], OP[hs, par, 2, csl], OP[hs, par, 1, csl], False, True), r=[kOP[2], kOP[1]], w=[kA])
                        S.pe(I_mm(gA[:, 3, :], OP[hs, par, 3, csl], OP[hs, par, 1, csl], False, True), r=[kOP[3], kOP[1]], w=[kA])
                        S.dve(I_tt(AT[:, par, h, c, 0:2, :], gA[:, 0:2, :], bc(msu[:], [128, 2, 128], 1), ALU.mult), r=[kA, "m_su"], w=[kAT])
                        S.dve(I_tt(AT[:, par, h, c, 2:4, :], gA[:, 2:4, :], bc(mui[:], [128, 2, 128], 1), ALU.mult), r=[kA, "m_ui"], w=[kAT])
                        S.pe(I_mm(pb3[:, bL, csl], OP[hs, par, 0, csl], OP[hs, par, 2, csl], c == 0, True), r=[kOP[0], kOP[2]], w=[("PS", "b", bL)])
                    S.dve(I_tt(PA[:, par, h, 0, :, :], pb3[:, bL, :].rearrange("p (c t) -> p c t", t=128), bc(msl[:], [128, 4, 128], 1), ALU.mult),
                          r=[("PS", "b", bL), "m_sl"], w=[("PA", par, h, 0)])
                    S.dve(I_tt(XX[:, par, h, :, :], AT[:, par, h, :, 0, :], bc(idt[:], [128, 4, 128], 1), ALU.add), r=[kAT, "idt"], w=[("XX", par, h)])
                    for k in range(1, 7):
                        pi, po = (k - 1) % 2, k % 2
                        bP = nxt("nb")
                        for c in range(4):
                            csl = slice(c * 128, (c + 1) * 128)
                            ptk = (AT[:, par, h, c, 0, :] if k == 1 else PTA[:, par, h, pi, c, :])
                            S.pe(I_mm(pb3[:, bP, csl], ptk, PA[:, par, h, pi, c, :], c == 0, True), r=[kAT, ("PTA", par, h, pi), ("PA", par, h, pi)], w=[("PS", "b", bP)])
                        S.act(I_act(PA[:, par, h, po, :, :], pb3[:, bP, :].rearrange("p (c t) -> p c t", t=128), AF.Copy), r=[("PS", "b", bP)], w=[("PA", par, h, po)])
                        if k < 6:
                            bT = nxt("nb")
                            for c in range(4):
                                csl = slice(c * 128, (c + 1) * 128)
                                ptk = (AT[:, par, h, c, 0, :] if k == 1 else PTA[:, par, h, pi, c, :])
                                S.pe(I_mm(pb3[:, bT, csl], PA[:, par, h, pi, c, :], ptk, c == 0, True), r=[kAT, ("PTA", par, h, pi), ("PA", par, h, pi)], w=[("PS", "b", bT)])
                            if k % 2 == 1:
                                S.dve(I_cp(PTA[:, par, h, po, :, :], pb3[:, bT, :].rearrange("p (c t) -> p c t", t=128)), r=[("PS", "b", bT)], w=[("PTA", par, h, po)])
                            else:
                                S.act(I_act(PTA[:, par, h, po, :, :], pb3[:, bT, :].rearrange("p (c t) -> p c t", t=128), AF.Copy), r=[("PS", "b", bT)], w=[("PTA", par, h, po)])
                        bX = nxt("nb")
                        for _ in range(NDUM3):
                            S.pe(I_mm(pb3[:, bX, :], idt[:], OP[:, par, 0, :], True, True), r=[kOP[0]], w=[("PS", "b", bX)])
                        for c in range(4):
                            csl = slice(c * 128, (c + 1) * 128)
                            S.pe(I_mm(pb3[:, bX, csl], PA[:, par, h, po, c, :], XX[:, par, h, c, :], c == 0, True), r=[("PA", par, h, po), ("XX", par, h)], w=[("PS", "b", bX)])
                        S.dve(I_tt(XX[:, par, h, :, :], pb3[:, bX, :].rearrange("p (c t) -> p c t", t=128), XX[:, par, h, :, :], ALU.add),
                              r=[("PS", "b", bX), ("XX", par, h)], w=[("XX", par, h)])
                S = S_pair
                S.ops.extend(merge2(heads[0].ops, heads[1].ops))
                for c in range(4):
                    csl = slice(c * 128, (c + 1) * 128)
                    ub = c % 2
                    b0 = BG
                    k0_ = ("PS", "b", b0)
                    for h in range(2):
                        hs = slice(h * 64, (h + 1) * 64)
                        S.pe(I_mm(pb3[:, b0, hs], OP[hs, par, 0, csl], Hb[hs, j, :], h == 0, False), r=[kOP[0], ("Hb", j)], w=[k0_])
                        S.pe(I_mm(pb3[:, b0, hs], AT[:, par, h, c, 1, :], TM[:, par, c, 2, hs], False, True), r=[("AT", par, h), ("TM", par, c)], w=[k0_])
                    S.act(I_act(U0b[:, par, ub, :], pb3[:, b0, 0:128], AF.Copy), r=[k0_], w=[("U0b", par, ub)])
                    b1 = BT
                    k1_ = ("PS", "b", b1)
                    for h in range(2):
                        hs = slice(h * 64, (h + 1) * 64)
                        S.pe(I_mm(pb3[:, b1, hs], XX[:, par, h, c, :], U0b[:, par, ub, hs], h == 0, True), r=[("XX", par, h), ("U0b", par, ub)], w=[k1_])
                    S.act(I_act(Ub[:, par, ub, :], pb3[:, b1, 0:128], AF.Copy), r=[k1_], w=[("Ub", par, ub)])
                    b3 = BN1
                    k3_ = ("PS", "b", b3)
                    for h in range(2):
                        hs = slice(h * 64, (h + 1) * 64)
                        S.pe(I_mm(pb3[hs, b3, 0:64], TM[:, par, c, 1, hs], TM[:, par, c, 2, hs], True, False), r=[("TM", par, c)], w=[k3_])
                    for h in range(2):
                        hs = slice(h * 64, (h + 1) * 64)
                        S.pe(I_mm(pb3[hs, b3, 0:64], TM[:, par, c, 0, hs], Ub[:, par, ub, hs], False, True), r=[("TM", par, c), ("Ub", par, ub)], w=[k3_])
                    b2 = BN0
                    k2_ = ("PS", "b", b2)
                    for h in range(2):
                        hs = slice(h * 64, (h + 1) * 64)
                        S.pe(I_mm(pb3[:, b2, hs], OP[hs, par, 1, csl], Hb[hs, j, :], h == 0, False), r=[kOP[1], ("Hb", j)], w=[k2_])
                        S.pe(I_mm(pb3[:, b2, hs], AT[:, par, h, c, 3, :], TM[:, par, c, 2, hs], False, False), r=[("AT", par, h), ("TM", par, c)], w=[k2_])
                        S.pe(I_mm(pb3[:, b2, hs], AT[:, par, h, c, 2, :], Ub[:, par, ub, hs], False, True), r=[("AT", par, h), ("Ub", par, ub)], w=[k2_])
                    S.act(I_act(P3[:, par, 0, 8 + c * 128:8 + (c + 1) * 128], pb3[:, b2, 0:128], AF.Copy), r=[k2_], w=[kP[0]])
                    S.dve(I_stt(Hf[:, j, :], Hf[:, j, :], GC[:, par, c:c + 1], pb3[:, b3, 0:64], ALU.mult, ALU.add),
                          r=[k3_, ("Hf", j), ("GC", par)], w=[("Hf", j)])
                    S.act(I_act(Hb[:, j, :], Hf[:, j, :], AF.Copy), r=[("Hf", j)], w=[("Hb", j)])
                Y3 = P3[:, par, 0, 8:520].rearrange("p (a d) -> p a d", d=64)
                G1v = D3[:, par, 0, :]
                G2v = D3[:, par, 1, :]
                g1 = G1v.rearrange("p (a d) -> p a d", d=64)
                kY = kP[0]
                kG1, kG2 = kD[0], kD[1]
                s1, s2, s3 = st1[:, par, :], st2[:, par, :], st3[:, par, :]
                ks1, ks2, ks3 = ("st1", par), ("st2", par), ("st3", par)
                S.dve(lambda e: e.tensor_reduce(out=s1, in_=Y3, axis=AX.X, op=ALU.add), r=[kY], w=[ks1])
                S.act(I_act(g1, Y3, AF.Square), r=[kY], w=[kG1])
                S.dve(lambda e: e.tensor_reduce(out=s2, in_=g1, axis=AX.X, op=ALU.add), r=[kG1], w=[ks2])
                S.dve(I_ts(s1, s1, 1.0 / 64, None, ALU.mult), r=[ks1], w=[ks1])
                S.dve(I_tt(s3, s1, s1, ALU.mult), r=[ks1], w=[ks3])
                S.dve(I_stt(s2, s2, 1.0 / 64, s3, ALU.mult, ALU.subtract), r=[ks2, ks3], w=[ks2])
                S.act(I_act(s2, s2, AF.Ln, bias=64e-5), r=[ks2], w=[ks2])
                S.act(I_act(s2, s2, AF.Exp, scale=-0.5), r=[ks2], w=[ks2])
                S.dve(I_tt(g1, Y3, bc(s1, [128, 8, 64], 2), ALU.subtract), r=[kY, ks1, kG1], w=[kG1])
                S.dve(I_tt(g1, g1, bc(s2, [128, 8, 64], 2), ALU.mult), r=[kG1, ks2], w=[kG1])
                g1c = G1v.rearrange("p (c f) -> p c f", f=128)
                S.dve(I_tt(g1c, g1c, bc(gnw[:, j * 128:(j + 1) * 128], [128, 4, 128], 1), ALU.mult), r=[kG1, "gnwbc"], w=[kG1])
                S.dve(I_tt(g1c, g1c, bc(gnb[:, j * 128:(j + 1) * 128], [128, 4, 128], 1), ALU.add), r=[kG1, "gnbbc"], w=[kG1])
                V3 = TM[:, par, :, 2, :].rearrange("p c (h d) -> p c h d", d=64)
                S.dve(I_tt(G2v.rearrange("p (c h d) -> p c h d", h=2, d=64), V3, bc(RK[:, par, :, :], [128, 4, 2, 64], 3), ALU.mult),
                      r=[("TM", par, 0), ("TM", par, 1), ("TM", par, 2), ("TM", par, 3), ("RK", par)], w=[kG2])
                S.dve(I_tt(G1v, G1v, G2v, ALU.add), r=[kG1, kG2], w=[kG1])
                msl_ = mix[:, tb * 4:(tb + 1) * 4, 512 + j * 128:512 + (j + 1) * 128]
                S.dve(I_tt(msl_, g1c, msl_, ALU.mult), r=[kG1], w=[("mixr3", tb, j)])

            for tb in range(NTB):
                blk_head(tb)
            seqs = [[], []]
            allops = []
            for tb in range(NTB):
                for j in range(NJ):
                    Sp = Sched()
                    pair(tb, j, Sp)
                    seqs[(tb * 4 + j) % 2].extend(Sp.ops)
                    allops.extend(Sp.ops)
            if os.environ.get("KMERGE", "1") == "1":
                S.ops.extend(merge2(seqs[0], seqs[1], float(os.environ.get("KOFF", "0"))))
                print("phase3 est makespan", merge2.last_makespan)
            else:
                S.ops.extend(allops)
            dump(S, "mix3", mix[:], [128, NT, D], BF16, [("mixr3", tb_, j_) for tb_ in range(NTB) for j_ in range(NJ)])
            S.emit(nc, esems, dsems, swsems)
            print("phase3 ops", len(S.ops))
        if upto <= 4:
            return nc, dbg_out

        p4 = ExitStack()
        with p4:
            sb, ps, sems = mk(p4)
            S = Sched()
            esems, dsems, swsems = sems("p4", 2)
            wo = sb("wo", [128, 8, D], BF16)
            fg = sb("fg", [128, D], F32)
            NTH = 4
            mT = sb("mT", [128, NTH, 8, 128], BF16)
            xr_ = sb("xr4", [128, NTH, D], F32)
            yb = sb("yb", [128, NTH, D], F32)
            ob = sb("ob", [128, NTH, D], F32)
            sq = sb("sq4", [128, NTH, D], BF16)
            ss4 = sb("ss4", [128, NT], F32)
            ln4 = sb("ln4", [128, NT], F32)
            rs4 = sb("rs4", [128, NT], F32)
            pm = ps("pm", [128, NTH, 8, 128], BF16)
            po = ps("po", [128, NTH, 512], F32)
            for hf in range(2):
                S.dma(I_dma(wo[:, :, hf * 512:(hf + 1) * 512], dr["w_out"][:, :, hf * 512:(hf + 1) * 512]), w=[("wo", hf)], q="pool")
            S.dma(lambda e: e.dma_start(out=fg[:], in_=dr["fgbc"]), w=["fg"])
            S.pool(lambda e: e.memset(ss4[:], 0.0), w=["ss4"])

            def out_thread(tt, S_):
                b = tt % NTH
                S_.dma(I_dma(xr_[:, b, :], x[tt * 128:(tt + 1) * 128, :]), w=[("xr4", b)])
                for k in range(8):
                    S_.pe(I_tr(pm[:, b, k, :], mix[:, tt, k * 128:(k + 1) * 128], idt[:]), r=["idt"], w=[("PS", "pm", b)])
                S_.act(I_act(mT[:, b, :, :], pm[:, b, :, :], AF.Copy), r=[("PS", "pm", b)], w=[("mT", b)])
                for hf in range(2):
                    for k in range(8):
                        S_.pe(I_mm(po[:, b, :], mT[:, b, k, :], wo[:, k, hf * 512:(hf + 1) * 512], k == 0, k == 7),
                              r=[("mT", b), ("wo", hf)], w=[("PS", "po", b)])
                    S_.dve(I_tt(yb[:, b, hf * 512:(hf + 1) * 512], po[:, b, :], xr_[:, b, hf * 512:(hf + 1) * 512], ALU.add),
                           r=[("PS", "po", b), ("xr4", b)], w=[("yb", b, hf)])
                S_.act(I_act(sq[:, b, :], yb[:, b, :], AF.Square, accum_out=ss4[:, tt:tt + 1]),
                       r=[("yb", b, 0), ("yb", b, 1), "ss4"], w=[("sq4", b), ("ss4", tt)])
                S_.act(I_act(ln4[:, tt:tt + 1], ss4[:, tt:tt + 1], AF.Ln, scale=1.0 / D, bias=1e-6), r=[("ss4", tt)], w=[("ln4", tt)])
                S_.act(I_act(rs4[:, tt:tt + 1], ln4[:, tt:tt + 1], AF.Exp, scale=-0.5), r=[("ln4", tt)], w=[("rs4", tt)])
                S_.dve(I_stt(ob[:, b, :], yb[:, b, :], rs4[:, tt:tt + 1], fg[:], ALU.mult, ALU.mult),
                       r=[("yb", b, 0), ("yb", b, 1), ("rs4", tt), "fg"], w=[("ob", b)])
                S_.dma(I_dma(out[tt * 128:(tt + 1) * 128, :], ob[:, b, :]), r=[("ob", b)], w=[("out", tt)])

            seqs = [[] for _ in range(NTH)]
            for tt in range(NT):
                Sp = Sched()
                out_thread(tt, Sp)
                seqs[tt % NTH].extend(Sp.ops)
            S.ops.extend(merge2(merge2(seqs[0], seqs[1]), merge2(seqs[2], seqs[3])))
            S.emit(nc, esems, dsems, swsems)
            print("phase4 ops", len(S.ops))
    return nc, dbg_out


_CACHE = {}


def kernel(**inputs):
    if "nc" not in _CACHE:
        _CACHE["nc"] = build()[0]
        _CACHE["consts"] = host_consts()
    nc = _CACHE["nc"]
    shared = dict(_CACHE["consts"])
    shared.update(host_weights(inputs))
    xin = np.asarray(inputs["x"], dtype=np.float32)
    B = xin.shape[0]
    in_maps = []
    for b in range(B):
        m = dict(shared)
        m["x"] = np.ascontiguousarray(xin[b])
        in_maps.append(m)
    res = run_bass_kernel_spmd(nc, in_maps, core_ids=list(range(B)))
    return np.stack([np.asarray(r["out"], dtype=np.float32) for r in res.results], axis=0)
```

```python
from contextlib import ExitStack
import os
STG = int(os.environ.get('KSTG', '99'))
NDUM = int(os.environ.get('KDUM', '0'))
KLAT = float(os.environ.get('KLAT', '0.6'))
KPE = float(os.environ.get('KPE', '1.0'))
KAD = float(os.environ.get('KAD', '0.85'))
DUMN = int(os.environ.get('KDUMN', '256'))
NDUM3 = int(os.environ.get('KDUM3', '0'))
NDUM4 = int(os.environ.get('KDUM4', '0'))

import numpy as np
import ml_dtypes

import concourse.bass as bass
import concourse.mybir as mybir
from concourse.bass_utils import run_bass_kernel_spmd

F32 = mybir.dt.float32
BF16 = mybir.dt.bfloat16
AF = mybir.ActivationFunctionType
ALU = mybir.AluOpType
AX = mybir.AxisListType
NPBF = ml_dtypes.bfloat16

T = 2048
D = 1024
NT = 16
P = 128
NEG = -1.0e5
BIGSEL = 1.0e5


class Sched:
    BLOCK = {"pe": "tensor", "act": "scalar", "dve": "vector", "pool": "gpsimd", "sp": "sync"}

    def __init__(self):
        self.ops = []

    def add(self, eng, fn, r=(), w=(), dma=False):
        psr = tuple(k for k in r if isinstance(k, tuple) and k and k[0] == "PS")
        r = tuple(r)
        w = tuple(w) + tuple(k for k in psr if k not in w)
        self.ops.append((eng, fn, tuple(r), tuple(w), dma))
        return len(self.ops) - 1

    def pe(self, fn, r=(), w=()):
        return self.add("pe", fn, r, w)

    def act(self, fn, r=(), w=()):
        return self.add("act", fn, r, w)

    def dve(self, fn, r=(), w=()):
        return self.add("dve", fn, r, w)

    def pool(self, fn, r=(), w=()):
        return self.add("pool", fn, r, w)

    def dma(self, fn, r=(), w=(), q="sp"):
        return self.add(q, fn, r, w, dma=True)

    def emit(self, nc, esems, dsems, swsems=()):
        ops = self.ops
        ndma = [i for i in range(len(ops)) if ops[i][4]]
        ops.append(("sp", None, (), (), False))
        n = len(ops)
        last_w = {}
        readers = {}
        deps = [dict() for _ in range(n)]
        for i, (eng, fn, rd, wr, dma) in enumerate(ops):
            for k in rd:
                j = last_w.get(k)
                if j is not None and j != i:
                    deps[i][j] = True
            for k in wr:
                j = last_w.get(k)
                if j is not None and j != i:
                    deps[i].setdefault(j, False)
                for r in readers.get(k, ()):
                    if r != i:
                        deps[i].setdefault(r, False)
            for k in rd:
                readers.setdefault(k, []).append(i)
            for k in wr:
                last_w[k] = i
                readers[k] = []
        kept = [[] for _ in range(n)]
        signal = [bool(op[4]) for op in ops]
        for i in range(n):
            ei, _, _, _, di = ops[i]
            for j, raw in deps[i].items():
                ej, _, _, _, dj = ops[j]
                if not dj and not di and ej == ei:
                    if ei == "pe":
                        continue
                kept[i].append(j)
                signal[j] = True
        kept[n - 1] = list(ndma)
        ecount = {e: 0 for e in self.BLOCK}
        dcount = [0] * len(dsems)
        dnext = 0
        swnext = 0
        sig = [None] * n
        for i in range(n):
            if not signal[i]:
                continue
            e, _, _, _, dma = ops[i]
            if dma and e == "pool":
                sig[i] = (swsems[swnext], 16, 16)
                swnext += 1
            elif dma:
                s = dnext % len(dsems)
                dnext += 1
                dcount[s] += 16
                sig[i] = (dsems[s], dcount[s], 16)
            else:
                ecount[e] += 1
                sig[i] = (esems[e], ecount[e], 1)
        waited = {e: {} for e in self.BLOCK}
        waits = [[] for _ in range(n)]
        for i in range(n):
            e = ops[i][0]
            need = {}
            for j in kept[i]:
                sem, val, _ = sig[j]
                key = id(sem)
                if val > need.get(key, (None, 0))[1]:
                    need[key] = (sem, val)
            if ops[i][4] and e != "pool" and sig[i][1] > 16:
                sem, val, _ = sig[i]
                if val - 16 > need.get(id(sem), (None, 0))[1]:
                    need[id(sem)] = (sem, val - 16)
            for key, (sem, val) in need.items():
                if waited[e].get(key, 0) >= val:
                    continue
                waited[e][key] = val
                waits[i].append((sem, val))
        with nc.Block() as block:
            for e, bname in self.BLOCK.items():
                idx = [i for i in range(n) if ops[i][0] == e]
                if not idx:
                    continue

                def body(engine, idx=idx):
                    for i in idx:
                        for sem, val in waits[i]:
                            engine.wait_ge(sem, val)
                        fn = ops[i][1]
                        if fn is None:
                            continue
                        ins = fn(engine)
                        if signal[i]:
                            ins.then_inc(sig[i][0], sig[i][2])

                getattr(block, bname)(body)


def _fd(ap):
    n = 1
    for d in ap.shape[1:]:
        n *= int(d)
    return n


def _c(fn, cost):
    fn.cost = cost
    return fn


def I_mm(out, lhsT, rhs, start=True, stop=True):
    return _c(lambda e: e.matmul(out, lhsT=lhsT, rhs=rhs, start=start, stop=stop, skip_group_check=True),
              0.08 + max(_fd(out), 64) / 2400.0 * (4 if rhs.dtype == F32 else 1))


def I_tr(out, in_, ident):
    return _c(lambda e: e.transpose(out=out, in_=in_, identity=ident), 0.1)


def I_act(out, in_, func, **kw):
    return _c(lambda e: e.activation(out=out, in_=in_, func=func, **kw), 0.2 + _fd(out) / 1200.0)


def I_tt(out, in0, in1, op):
    return _c(lambda e: e.tensor_tensor(out=out, in0=in0, in1=in1, op=op), 0.1 + _fd(out) / 960.0)


def I_ts(out, in0, s1, s2, op0, op1=None):
    if op1 is None:
        return _c(lambda e: e.tensor_scalar(out=out, in0=in0, scalar1=s1, scalar2=None, op0=op0), 0.1 + _fd(out) / 960.0)
    return _c(lambda e: e.tensor_scalar(out=out, in0=in0, scalar1=s1, scalar2=s2, op0=op0, op1=op1), 0.1 + _fd(out) / 960.0)


def I_stt(out, in0, scalar, in1, op0, op1):
    return _c(lambda e: e.scalar_tensor_tensor(out=out, in0=in0, scalar=scalar, in1=in1, op0=op0, op1=op1), 0.1 + _fd(out) / 960.0)


def I_cp(out, in_):
    return _c(lambda e: e.tensor_copy(out=out, in_=in_), 0.1 + _fd(out) / 960.0)


def I_rcp(out, in_):
    return _c(lambda e: e.reciprocal(out=out, in_=in_), 0.1 + _fd(out) / 960.0)


def I_ms(out, val):
    return _c(lambda e: e.memset(out, val), 0.1 + _fd(out) / 960.0)


def I_dma(out, in_):
    return _c(lambda e: e.dma_start(out=out, in_=in_), 2.0)


_COST = {"pe": 0.12, "act": 0.65, "dve": 0.7, "pool": 1.0, "sp": 2.0}


def merge2(A, B, offset_b=0.0):
    seqs = [A, B]
    idx = [0, 0]
    eng_free = {}
    kw = [{}, {}]
    kr = [{}, {}]
    out = []

    def ready(si):
        eng, fn, r, w, dma = seqs[si][idx[si]]
        t = max(eng_free.get(eng, 0.0), offset_b if si == 1 else 0.0)
        for k in r:
            tw, ew = kw[si].get(k, (0.0, eng))
            t = max(t, tw + (KLAT if ew != eng else 0.05))
        for k in w:
            tw, ew = kw[si].get(k, (0.0, eng))
            t = max(t, tw + (KLAT if ew != eng else 0.05))
            tr_, er = kr[si].get(k, (0.0, eng))
            t = max(t, tr_ + (KLAT if er != eng else 0.0))
        return t

    while idx[0] < len(A) or idx[1] < len(B):
        cands = [si for si in (0, 1) if idx[si] < len(seqs[si])]
        if len(cands) == 2:
            t0, t1 = ready(0), ready(1)
            if t0 < t1 or (t0 == t1 and idx[0] * len(B) <= idx[1] * len(A)):
                si = 0
            else:
                si = 1
        else:
            si = cands[0]
        op = seqs[si][idx[si]]
        t = ready(si)
        eng = op[0]
        cost = getattr(op[1], "cost", None)
        if cost is None:
            cost = _COST.get(eng, 0.5)
        if eng == "pool":
            cost = 0.2 + 1.6 * cost
        if eng == "pe":
            cost = cost * KPE
        elif eng in ("act", "dve"):
            cost = cost * KAD
        te = t + cost
        eng_free[eng] = te
        for k in op[3]:
            kw[si][k] = (te, eng)
        for k in op[2]:
            if te > kr[si].get(k, (0.0, eng))[0]:
                kr[si][k] = (te, eng)
        out.append(op)
        idx[si] += 1
    merge2.last_makespan = max(eng_free.values()) if eng_free else 0.0
    return out


def bc(ap, shape, axis):
    return ap.unsqueeze(axis).broadcast_to(shape)


def host_consts():
    c = {}
    c["ident"] = np.eye(128, dtype=np.float32).astype(NPBF)
    c["identf"] = np.eye(128, dtype=np.float32)
    pos = np.arange(T, dtype=np.float32)
    half = 8
    inv = (500000.0 ** (-np.arange(half, dtype=np.float32) / half)).astype(np.float32)
    ang = pos[:, None] * inv[None, :]
    cos = np.cos(ang).astype(np.float32).reshape(NT, 128, half).transpose(1, 0, 2)
    sin = np.sin(ang).astype(np.float32).reshape(NT, 128, half).transpose(1, 0, 2)
    c["ropec"] = np.ascontiguousarray(cos)
    c["ropes"] = np.ascontiguousarray(sin)
    E = (np.arange(T)[None, :] // 64 == np.arange(32)[:, None]).astype(np.float32)
    c["emat"] = E.astype(NPBF)
    c0 = np.arange(127)[:, None] * 16
    s0 = np.arange(32)[None, :] * 64
    ov = np.clip(np.minimum(c0 + 32, s0 + 64) - np.maximum(c0, s0), 0, None) / 32.0
    c["mcs"] = ov.astype(np.float32).astype(NPBF)
    cend = np.arange(127)[:, None] * 16 + 31
    c["cmpb"] = np.where(cend <= np.arange(T)[None, :], 0.0, NEG).astype(np.float32).astype(NPBF)
    kk = np.arange(128)[:, None]
    qq = np.arange(128)[None, :]
    c["cb"] = np.where(kk > qq, NEG, 0.0).astype(np.float32).astype(NPBF)
    c["acb"] = np.where(kk > qq, 0.0, NEG).astype(np.float32).astype(NPBF)
    t = np.arange(T)
    tb = (t // 64)[:, None]
    j = np.arange(32)[None, :]
    valid = (j <= tb).astype(np.float32)
    forced = ((j == 0) | (j == tb) | (j == tb - 1)).astype(np.float32)
    fv = 1000.0 * forced * valid + valid - 1.0
    c["valid"] = np.ascontiguousarray(valid.reshape(NT, 128, 32).transpose(1, 0, 2))
    c["fv"] = np.ascontiguousarray(fv.reshape(NT, 128, 32).transpose(1, 0, 2).astype(np.float32))
    rm = np.ones((128, 512), np.float32)
    rm[:, ::128] = 0.0
    c["rmask"] = rm.astype(NPBF)
    bd = np.zeros((128, 128), np.float32)
    bd[:64, :64] = 1.0
    bd[64:, 64:] = 1.0
    c["onesbd"] = bd
    ind = np.zeros((128, 2), np.float32)
    ind[:64, 0] = 1.0
    ind[64:, 1] = 1.0
    c["ind2"] = ind.astype(NPBF)
    pp_ = np.arange(128)[:, None]
    ff_ = np.arange(128)[None, :]
    c["m_su"] = (ff_ > pp_).astype(np.float32).astype(NPBF)
    c["m_ui"] = (ff_ >= pp_).astype(np.float32).astype(NPBF)
    c["m_sl"] = (ff_ < pp_).astype(np.float32).astype(NPBF)
    return c


IN_SIZES = (512, 128, 128, 128, 128, 128, 128, 24, 512, 1664, 512)
OFFS = np.concatenate([[0], np.cumsum(IN_SIZES)]).tolist()
(O_Q, O_KC, O_VC, O_KS, O_VS, O_KW, O_VW, O_GL, O_GN, O_RW, O_GR, O_END) = OFFS


def host_weights(inp):
    w = {}
    w_in = np.asarray(inp["w_in"][0], dtype=np.float32)

    def cols(*rngs):
        m = np.concatenate([w_in[:, a:b] for a, b in rngs], axis=1)
        return np.ascontiguousarray(m.reshape(8, 128, -1).transpose(1, 0, 2))

    w["w_rope"] = cols((O_Q, O_Q + 512), (O_KS, O_KS + 128), (O_KW, O_KW + 128))
    w["w_v"] = cols((O_VS, O_VS + 128), (O_VW, O_VW + 128))
    w["w_gn"] = cols((O_GN, O_GN + 512))
    w["w_gr"] = cols((O_GR, O_GR + 512))
    w["w_gl"] = cols((O_GL, O_GL + 24))
    w["w_kvc"] = cols((O_KC, O_KC + 128), (O_VC, O_VC + 128))
    w["w_rw"] = cols((O_RW, O_RW + 1664))
    w["gbc"] = np.ascontiguousarray(np.broadcast_to(np.asarray(inp["norm_g"][0], np.float32)[None, :], (128, D)))
    w["fgbc"] = np.ascontiguousarray(np.broadcast_to(np.asarray(inp["final_g"], np.float32)[None, :], (128, D)))
    for kind in ("k", "v"):
        w1 = np.asarray(inp["cmp_w1_" + kind][0], np.float32)
        w["w1" + kind] = np.ascontiguousarray(w1.reshape(32, 64, 256).transpose(1, 0, 2))
        w2 = np.asarray(inp["cmp_w2_" + kind][0], np.float32)
        w["w2" + kind] = np.ascontiguousarray(w2.reshape(2, 128, 64).transpose(1, 0, 2))
        pe = np.asarray(inp["cmp_pos_" + kind][0], np.float32)
        w["pos" + kind] = np.ascontiguousarray(pe.T)
    w["mu13"] = np.ascontiguousarray(np.asarray(inp["shift_mu"][0], np.float32).reshape(13, 128).T)
    for nm, key in (("w0c", "decay_w0"), ("a0c", "iclr_a0"), ("kkc", "k_k"), ("kac", "k_a"), ("rkc", "r_k")):
        w[nm] = np.ascontiguousarray(np.asarray(inp[key][0], np.float32).reshape(4, 128).T)
    w["wup"] = np.ascontiguousarray(np.concatenate([np.asarray(inp["decay_up"][0], np.float32),
                                                    np.asarray(inp["iclr_up"][0], np.float32)], axis=0))
    w["gnwbc"] = np.ascontiguousarray(np.broadcast_to(np.asarray(inp["gn_w"][0], np.float32)[None, :], (128, 512)))
    w["gnbbc"] = np.ascontiguousarray(np.broadcast_to(np.asarray(inp["gn_b"][0], np.float32)[None, :], (128, 512)))
    w_out = np.asarray(inp["w_out"][0], np.float32)
    w["w_out"] = np.ascontiguousarray(w_out.reshape(8, 128, D).transpose(1, 0, 2))
    return w


def build(debug=(), upto=99):
    nc = bass.Bass("TRN2", target_bir_lowering=False)
    dr = {}

    def din(name, shape, dt=F32):
        dr[name] = nc.dram_tensor(name, list(shape), dt, kind="ExternalInput").ap()
        return dr[name]

    def dout(name, shape, dt=F32):
        dr[name] = nc.dram_tensor(name, list(shape), dt, kind="ExternalOutput").ap()
        return dr[name]

    x = din("x", [T, D])
    out = dout("out", [T, D])
    for nm, shp in (("w_rope", [128, 8, 768]), ("w_v", [128, 8, 256]), ("w_gn", [128, 8, 512]),
                    ("w_gr", [128, 8, 512]), ("w_gl", [128, 8, 24]), ("w_kvc", [128, 8, 256]),
                    ("w_rw", [128, 8, 1664]), ("gbc", [128, D]), ("fgbc", [128, D]),
                    ("w1k", [64, 32, 256]), ("w1v", [64, 32, 256]), ("w2k", [128, 2, 64]),
                    ("w2v", [128, 2, 64]), ("posk", [64, 32]), ("posv", [64, 32]),
                    ("w_out", [128, 8, D]), ("ropec", [128, NT, 8]), ("ropes", [128, NT, 8]),
                    ("valid", [128, NT, 32]), ("fv", [128, NT, 32]), ("identf", [128, 128]),
                    ("mu13", [128, 13]), ("w0c", [128, 4]), ("a0c", [128, 4]), ("kkc", [128, 4]), ("kac", [128, 4]),
                    ("rkc", [128, 4]), ("wup", [128, 512]), ("gnwbc", [128, 512]), ("gnbbc", [128, 512]),
                    ("onesbd", [128, 128])):
        din(nm, shp)
    for nm, shp in (("ident", [128, 128]), ("emat", [32, T]), ("mcs", [127, 32]), ("cmpb", [127, T]),
                    ("cb", [128, 128]), ("acb", [128, 128]), ("rmask", [128, 512]), ("ind2", [128, 2]),
                    ("m_su", [128, 128]), ("m_ui", [128, 128]), ("m_sl", [128, 128])):
        din(nm, shp, BF16)

    dbg_out = {}

    outer = ExitStack()
    with outer:
        def mk(es):
            def sb(name, shape, dt):
                return es.enter_context(nc.sbuf_tensor(name, list(shape), dt))

            def ps(name, shape, dt):
                return es.enter_context(nc.psum_tensor(name, list(shape), dt))

            def sems(tag, nsw=0, nd=8):
                esems = {e: outer.enter_context(nc.semaphore(tag + "_" + e)) for e in Sched.BLOCK}
                dsems = [outer.enter_context(nc.semaphore("%s_d%d" % (tag, i))) for i in range(nd)]
                swsems = [outer.enter_context(nc.semaphore("%s_w%d" % (tag, i))) for i in range(nsw)]
                return esems, dsems, swsems
            return sb, ps, sems

        sb0, ps0, sems0 = mk(outer)
        xT = sb0("xT", [128, 8, T], BF16)
        mix = sb0("mix", [128, NT, D], BF16)
        idt = sb0("idt", [128, 128], BF16)

        def dump(S, name, tens_ap, shape, dt, rkeys):
            if name not in debug:
                return
            d = dout("dbg_" + name, shape, dt)
            dbg_out[name] = d
            S.dma(lambda e: e.dma_start(out=d, in_=tens_ap), r=rkeys, w=["dbg_" + name])
            S.add("sp", None, r=["dbg_" + name])

        nsa = ExitStack()
        with nsa:
            sb1, ps1, sems1 = mk(nsa)
            QG = sb1("QG", [96, 2, 4, T], BF16)
            QU = sb1("QU", [64, 2, 4, T], BF16)
            KE = sb1("KE", [96, 2, T], BF16)
            KW = sb1("KW", [64, 2, T], BF16)
            VP = sb1("VP", [128, NT, 4, 65], BF16)
            GL = sb1("GL", [128, NT, 24], F32)
            KcT = sb1("KcT", [64, 2, 128], BF16)
            VcM = sb1("VcM", [128, 2, 97], BF16)

            p1 = ExitStack()
            with p1:
                sb, ps, sems = mk(p1)
                S = Sched()
                esems, dsems, swsems = sems("p1", 5)
                xs = sb("xs", [128, 2, D], F32)
                xb = sb("xb", [128, 2, D], BF16)
                gbc = sb("gbc_s", [128, D], F32)
                ss = sb("ss", [128, NT], F32)
                lnt = sb("lnt", [128, NT], F32)
                rstd = sb("rstd", [128, NT], F32)
                wbuf = sb("wbuf", [128, 8 * 768 + 8 * 512], BF16)
                qku = sb("qku", [128, 2, 512], BF16)
                qkk = sb("qkk", [128, 2, 768], F32)
                qkr = sb("qkr", [128, 2, 768], BF16)
                ra = sb("ra", [128, 2, 96], F32)
                rb = sb("rb", [128, 2, 96], F32)
                rc = sb("rc", [128, 2, 96], F32)
                rd_ = sb("rd", [128, 2, 96], F32)
                ropec = sb("ropec_s", [128, NT, 8], F32)
                ropes = sb("ropes_s", [128, NT, 8], F32)
                pt = ps("pt", [128, 2, 8, 128], BF16)
                pp = ps("pp", [128, 4, 512], F32)
                ptq = ps("ptq", [64, 2, 8, 128], BF16)

                S.pool(lambda e: e.memset(VP[:, :, :, 64:65], 1.0), w=["VPone"])
                S.pool(lambda e: e.memset(ss[:], 0.0), w=["ss"])

                wgroups = [("w_rope", 768), ("w_v", 256), ("w_gn", 512), ("w_gr", 512), ("w_gl", 24)]
                wslot = {}

                def wview(slot, ncols):
                    o0 = 0 if slot == 0 else 8 * 768
                    return wbuf[:, o0:o0 + 8 * ncols].rearrange("p (k n) -> p k n", k=8)

                def load_w(gi, after=()):
                    nm, ncols = wgroups[gi]
                    slot = gi % 2
                    wslot[nm] = slot
                    S.dma(lambda e: e.dma_start(out=wview(slot, ncols), in_=dr[nm]), r=list(after), w=[("wbuf", slot)], q="pool")

                ppc = [0, 0]

                def proj_tok2(S_, tt, wnm, ncols, c0, c1):
                    b_ = tt % 2
                    bk = 2 * b_ + ppc[b_] % 2
                    ppc[b_] += 1
                    wv = wview(wslot[wnm], ncols)
                    for k in range(8):
                        S_.pe(I_mm(pp[:, bk, 0:c1 - c0], xT[:, k, tt * 128:(tt + 1) * 128], wv[:, k, c0:c1], k == 0, k == 7),
                              r=[("xT", tt), ("wbuf", wslot[wnm])], w=[("PS", "pp", bk)])
                    return bk

                def tile_thread(tt, S_):
                    b = tt % 2
                    tsl = slice(tt * 128, (tt + 1) * 128)
                    if tt >= 2:
                        S_.dma(I_dma(xs[:, b, :], x[tt * 128:(tt + 1) * 128, :]), w=[("xs", b)])
                    S_.act(I_act(xb[:, b, :], xs[:, b, :], AF.Square, accum_out=ss[:, tt:tt + 1]),
                           r=[("xs", b), "ss"], w=[("xb", b), ("ss", tt)])
                    S_.act(I_act(lnt[:, tt:tt + 1], ss[:, tt:tt + 1], AF.Ln, scale=1.0 / D, bias=1e-6), r=[("ss", tt)], w=[("lnt", tt)])
                    S_.act(I_act(rstd[:, tt:tt + 1], lnt[:, tt:tt + 1], AF.Exp, scale=-0.5), r=[("lnt", tt)], w=[("rstd", tt)])
                    S_.dve(I_stt(xb[:, b, :], xs[:, b, :], rstd[:, tt:tt + 1], gbc[:], ALU.mult, ALU.mult),
                           r=[("xs", b), ("rstd", tt), "gbc"], w=[("xb", b)])
                    for k in range(8):
                        S_.pe(I_tr(pt[:, b, k, :], xb[:, b, k * 128:(k + 1) * 128], idt[:]), r=[("xb", b), "idt"], w=[("PS", "pt", b)])
                    S_.act(I_act(xT[:, :, tsl], pt[:, b, :, :], AF.Copy), r=[("PS", "pt", b)], w=[("xT", tt)])
                    if STG < 2:
                        return
                    bk0 = proj_tok2(S_, tt, "w_rope", 768, 0, 512)
                    bk1 = proj_tok2(S_, tt, "w_rope", 768, 512, 768)
                    S_.act(I_act(qkk[:, b, 0:512], pp[:, bk0, :], AF.Copy), r=[("PS", "pp", bk0)], w=[("qkk", b)])
                    S_.dve(I_cp(qkk[:, b, 512:768], pp[:, bk1, 0:256]), r=[("PS", "pp", bk1)], w=[("qkk", b)])
                    q3 = qkk[:, b, :].rearrange("p (h d) -> p h d", d=64)
                    o3 = qkr[:, b, :].rearrange("p (h d) -> p h d", d=64)
                    cosb = bc(ropec[:, tt, :], [128, 12, 8], 1)
                    sinb = bc(ropes[:, tt, :], [128, 12, 8], 1)
                    ra3 = ra[:, b, :].rearrange("p (h d) -> p h d", d=8)
                    rb3 = rb[:, b, :].rearrange("p (h d) -> p h d", d=8)
                    rc3 = rc[:, b, :].rearrange("p (h d) -> p h d", d=8)
                    rd3 = rd_[:, b, :].rearrange("p (h d) -> p h d", d=8)
                    S_.dve(I_tt(ra3, q3[:, :, 0:8], cosb, ALU.mult), r=[("qkk", b), "ropec"], w=[("ra", b)])
                    S_.dve(I_tt(rb3, q3[:, :, 8:16], sinb, ALU.mult), r=[("qkk", b), "ropes"], w=[("rb", b)])
                    S_.dve(I_tt(rc3, q3[:, :, 8:16], cosb, ALU.mult), r=[("qkk", b), "ropec"], w=[("rc", b)])
                    S_.dve(I_tt(rd3, q3[:, :, 0:8], sinb, ALU.mult), r=[("qkk", b), "ropes"], w=[("rd", b)])
                    S_.dve(I_tt(o3[:, :, 0:8], ra3, rb3, ALU.subtract), r=[("ra", b), ("rb", b)], w=[("qkr", b)])
                    S_.dve(I_tt(o3[:, :, 8:16], rc3, rd3, ALU.add), r=[("rc", b), ("rd", b)], w=[("qkr", b)])
                    S_.dve(I_cp(o3[:, :, 16:64], q3[:, :, 16:64]), r=[("qkk", b)], w=[("qkr", b)])
                    S_.act(I_act(qku[:, b, :], qkk[:, b, 0:512], AF.Copy), r=[("qkk", b)], w=[("qku", b)])
                    kq = ("PS", "ptq", b)
                    for h in range(8):
                        S_.pe(I_tr(ptq[:, b, h, :], qkr[:, b, h * 64:(h + 1) * 64], idt[:]), r=[("qkr", b), "idt"], w=[kq])
                    S_.act(I_act(QG[0:64, 0, :, tsl], ptq[:, b, 0:4, :], AF.Copy), r=[kq], w=[("QGq", 0, tt)])
                    S_.dve(I_cp(QG[0:64, 1, :, tsl], ptq[:, b, 4:8, :]), r=[kq], w=[("QGq", 1, tt)])
                    for h in range(4):
                        S_.pe(I_tr(ptq[:, b, h, :], qkr[:, b, (8 + h) * 64:(9 + h) * 64], idt[:]), r=[("qkr", b), "idt"], w=[kq])
                    S_.act(I_act(KE[0:64, :, tsl], ptq[:, b, 0:2, :], AF.Copy), r=[kq], w=[("KEk", tt)])
                    S_.dve(I_cp(KW[0:64, :, tsl], ptq[:, b, 2:4, :]), r=[kq], w=[("KWk", tt)])
                    for h in range(8):
                        S_.pe(I_tr(ptq[:, b, h, :], qku[:, b, h * 64:(h + 1) * 64], idt[:]), r=[("qku", b), "idt"], w=[kq])
                    S_.act(I_act(QU[:, 0, :, tsl], ptq[:, b, 0:4, :], AF.Copy), r=[kq], w=[("QU", 0, tt)])
                    S_.dve(I_cp(QU[:, 1, :, tsl], ptq[:, b, 4:8, :]), r=[kq], w=[("QU", 1, tt)])
                    if STG < 3:
                        return
                    bk = proj_tok2(S_, tt, "w_v", 256, 0, 256)
                    S_.act(I_act(VP[:, tt, :, 0:64], pp[:, bk, 0:256].rearrange("p (a d) -> p a d", d=64), AF.Copy),
                           r=[("PS", "pp", bk)], w=[("VP", tt)])

                S.dma(I_dma(xs[:, 0, :], x[0:128, :]), w=[("xs", 0)])
                S.dma(I_dma(idt[:], dr["ident"]), w=["idt"])
                S.dma(I_dma(gbc[:], dr["gbc"]), w=["gbc"])
                S.dma(I_dma(xs[:, 1, :], x[128:256, :]), w=[("xs", 1)])
                S.dma(I_dma(ropec[:], dr["ropec"]), w=["ropec"])
                S.dma(I_dma(ropes[:], dr["ropes"]), w=["ropes"])
                for g in range(2):
                    S.dma(I_dma(KE[64:96, g, :], dr["emat"]), w=[("KEm", g)])
                load_w(0, after=[("xs", 1), "gbc"])
                load_w(1, after=[("xs", 1), "gbc"])
                seqs = [[], []]
                for tt in range(NT):
                    Sp = Sched()
                    tile_thread(tt, Sp)
                    seqs[tt % 2].extend(Sp.ops)
                S.ops.extend(merge2(seqs[0], seqs[1]))
                load_w(2)
                load_w(3)
                ppn = [0]

                def proj_tok(tt, wnm, ncols, c0, c1):
                    bk = ppn[0] % 4
                    ppn[0] += 1
                    wv = wview(wslot[wnm], ncols)
                    for k in range(8):
                        S.pe(I_mm(pp[:, bk, 0:c1 - c0], xT[:, k, tt * 128:(tt + 1) * 128], wv[:, k, c0:c1], k == 0, k == 7),
                             r=[("xT", tt), ("wbuf", wslot[wnm])], w=[("PS", "pp", bk)])
                    return bk

                for tt in range(NT if STG >= 5 else 0):
                    bk = proj_tok(tt, "w_gn", 512, 0, 512)
                    S.act(lambda e, tt=tt, bk=bk: e.activation(out=mix[:, tt, 0:512], in_=pp[:, bk, :], func=AF.Silu),
                          r=[("PS", "pp", bk)], w=[("mixn", tt)])
                load_w(4)
                for tt in range(NT if STG >= 6 else 0):
                    bk = proj_tok(tt, "w_gr", 512, 0, 512)
                    S.act(lambda e, tt=tt, bk=bk: e.activation(out=mix[:, tt, 512:1024], in_=pp[:, bk, :], func=AF.Silu),
                          r=[("PS", "pp", bk)], w=[("mixr", tt)])
                for tt in range(NT if STG >= 7 else 0):
                    bk = proj_tok(tt, "w_gl", 24, 0, 24)
                    S.act(lambda e, tt=tt, bk=bk: e.activation(out=GL[:, tt, :], in_=pp[:, bk, 0:24], func=AF.Sigmoid),
                          r=[("PS", "pp", bk)], w=[("GL", tt)])

                allx = [("xT", t_) for t_ in range(NT)]
                dump(S, "xT", xT[:], [128, 8, T], BF16, allx)
                dump(S, "QG", QG[0:64], [64, 2, 4, T], BF16, [("QGq", g, t_) for g in range(2) for t_ in range(NT)])
                dump(S, "KE", KE[:], [96, 2, T], BF16, [("KEk", t_) for t_ in range(NT)] + [("KEm", 0), ("KEm", 1)])
                dump(S, "KW", KW[:], [64, 2, T], BF16, [("KWk", t_) for t_ in range(NT)])
                dump(S, "VP", VP[:], [128, NT, 4, 65], BF16, [("VP", t_) for t_ in range(NT)] + ["VPone"])
                dump(S, "GL", GL[:], [128, NT, 24], F32, [("GL", t_) for t_ in range(NT)])
                dump(S, "mix1", mix[:], [128, NT, D], BF16, [("mixn", t_) for t_ in range(NT)] + [("mixr", t_) for t_ in range(NT)])
                S.emit(nc, esems, dsems, swsems)
                print("phase1 ops", len(S.ops))

            if upto <= 1:
                return nc, dbg_out

            pc = ExitStack()
            with pc:
                sb, ps, sems = mk(pc)
                S = Sched()
                esems, dsems, swsems = sems("pc", 13)
                w1b = sb("w1b", [128, 2, 16, 256], BF16)
                kvT = sb("kvT", [128, 2, 16, 128], BF16)
                wkv = sb("wkv", [128, 8, 256], BF16)
                pp2 = ps("pp2", [128, 2, 512], F32)
                S.dma(lambda e: e.dma_start(out=wkv[:], in_=dr["w_kvc"]), w=["wkv"], q="pool")
                for ki in range(2):
                    for tb in range(4):
                        bk = (ki * 4 + tb) % 2
                        for k in range(8):
                            S.pe(I_mm(pp2[:, bk, :], wkv[:, k, ki * 128:(ki + 1) * 128], xT[:, k, tb * 512:(tb + 1) * 512], k == 0, k == 7),
                                 r=["wkv"], w=[("PS", "pp2", bk)])
                        dst = kvT[:, ki, :, tb * 32:(tb + 1) * 32].rearrange("p i c -> p c i")
                        src = pp2[:, bk, :].rearrange("p (c i) -> p c i", i=16)
                        if tb % 2 == 0:
                            S.act(I_act(dst, src, AF.Copy), r=[("PS", "pp2", bk)], w=[("kvT", ki)])
                        else:
                            S.dve(I_cp(dst, src), r=[("PS", "pp2", bk)], w=[("kvT", ki)])
                w2b = sb("w2b", [128, 2, 2, 64], BF16)
                posb = sb("posb", [64, 2, 32], BF16)
                hT = sb("hT", [128, 4, 2, 128], BF16)
                bias1 = sb("bias1", [128, 4], F32)
                pc1 = ps("pc1", [128, 8, 128], F32)
                pb = ps("pb", [128, 512], F32)
                pk_ = ps("pk", [128, 512], F32)
                pv_ = ps("pv", [128, 512], F32)
                pk = pk_[0:64, 0:256].rearrange("p (g c) -> p g c", c=128)
                pv = pv_[:, 0:128].rearrange("p (g c) -> p g c", c=64)
                for ki, kind in enumerate(("k", "v")):
                    S.dma(lambda e, ki=ki, kind=kind: e.dma_start(out=w2b[:, ki, :, :], in_=dr["w2" + kind]), w=[("w2b", ki)], q="pool")
                    S.dma(lambda e, ki=ki, kind=kind: e.dma_start(out=posb[:, ki, :], in_=dr["pos" + kind]), w=[("posb", ki)], q="pool")
                S.pool(lambda e: e.memset(VcM[:, :, 64:65], 1.0), w=["VcMone"])
                for g in range(2):
                    S.dma(lambda e, g=g: e.dma_start(out=VcM[0:127, g, 65:97], in_=dr["mcs"]), w=[("VcMm", g)])
                first_in_bank = {}
                for ki, kind in enumerate(("k", "v")):
                    for hb in range(2):
                        for half in range(2):
                            S.dma(I_dma(w1b[half * 64:(half + 1) * 64, hb, :, :], dr["w1" + kind][:, hb * 16:(hb + 1) * 16, :]),
                                  w=[("w1b", hb, half)], q="pool")
                        for il in range(16):
                            i = hb * 16 + il
                            for mc in range(2):
                                for g in range(2):
                                    gs = slice(g * 64, (g + 1) * 64)
                                    idx = g * 4 + ki * 2 + mc
                                    bank = ("PS", "pc1", g)
                                    st = (bank, ki) not in first_in_bank
                                    first_in_bank[(bank, ki)] = 1
                                    src = kvT[gs, ki, i, 0:127] if i < 16 else kvT[gs, ki, i - 16, 1:128]
                                    S.pe(I_mm(pc1[:, idx, 0:127], w1b[gs, hb, il, mc * 128:(mc + 1) * 128], src, st, i == 31),
                                         r=[("w1b", hb, g), ("kvT", ki)], w=[bank])
                                bank = ("PS", "pb")
                                st = bank not in first_in_bank
                                first_in_bank[bank] = 1
                                S.pe(I_mm(pb[:, ki * 2 + mc:ki * 2 + mc + 1], w1b[0:64, hb, il, mc * 128:(mc + 1) * 128], posb[:, ki, i:i + 1], st, i == 31),
                                     r=[("w1b", hb, 0), ("posb", ki)], w=[bank])
                S.dve(lambda e: e.tensor_copy(out=bias1[:], in_=pb[:, 0:4]), r=[("PS", "pb")], w=["bias1"])
                for j in range(4):
                    for mc in range(2):
                        ki = j // 2
                        g_ = j % 2
                        S.act(I_act(hT[:, j, mc, 0:127], pc1[:, g_ * 4 + ki * 2 + mc, 0:127], AF.Silu, bias=bias1[:, ki * 2 + mc:ki * 2 + mc + 1]),
                              r=[("PS", "pc1", g_), "bias1"], w=[("hT", j)])
                for g in range(2):
                    for mc in range(2):
                        S.pe(lambda e, g=g, mc=mc: e.matmul(pk[:, g, 0:127], lhsT=w2b[:, 0, mc, :], rhs=hT[:, g, mc, 0:127],
                                                            start=(mc == 0), stop=(mc == 1)),
                             r=[("w2b", 0), ("hT", g)], w=[("PS", "pk")])
                    S.act(lambda e, g=g: e.activation(out=KcT[:, g, 0:127], in_=pk[:, g, 0:127], func=AF.Copy),
                          r=[("PS", "pk")], w=[("KcT", g)])
                for g in range(2):
                    for mc in range(2):
                        S.pe(lambda e, g=g, mc=mc: e.matmul(pv[0:127, g, :], lhsT=hT[:, 2 + g, mc, 0:127], rhs=w2b[:, 1, mc, :],
                                                            start=(mc == 0), stop=(mc == 1)),
                             r=[("w2b", 1), ("hT", 2 + g)], w=[("PS", "pv")])
                    S.dve(lambda e, g=g: e.tensor_copy(out=VcM[0:127, g, 0:64], in_=pv[0:127, g, :]),
                          r=[("PS", "pv")], w=[("VcMv", g)])
                dump(S, "KcT", KcT[:, :, 0:127], [64, 2, 127], BF16, [("KcT", 0), ("KcT", 1)])
                dump(S, "VcM", VcM[0:127], [127, 2, 97], BF16, [("VcMv", 0), ("VcMv", 1), ("VcMm", 0), ("VcMm", 1), "VcMone"])
                S.emit(nc, esems, dsems, swsems)
                print("phase1c ops", len(S.ops))
            if upto <= 2:
                return nc, dbg_out

            p2 = ExitStack()
            with p2:
                sb, ps, sems = mk(p2)
                S = Sched()
                esems, dsems, swsems = sems("p2", 0)
                cmpb = sb("cmpb_s", [128, T], BF16)
                cb = sb("cb_s", [128, 128], BF16)
                acb = sb("acb_s", [128, 128], BF16)
                valid = sb("valid_s", [128, NT, 32], F32)
                fv = sb("fv_s", [128, NT, 32], F32)
                PT = sb("PT", [128, 4, 512], BF16)
                ocg = sb("ocg", [128, 2, 256], F32)
                selpad = sb("selpad", [128, 2, 96], BF16)
                dn = sb("dn", [128, 2, 4], F32)
                rden = sb("rden", [128, 2, 4], F32)
                coefc = sb("coefc", [128, 2, 4], F32)
                impa = sb("impa", [128, 2, 32], F32)
                imp2 = sb("imp2", [128, 2, 32], F32)
                m8 = sb("m8", [128, 2, 8], F32)
                dsw = sb("dsw", [128, 2, 8], F32)
                rsw = sb("rsw", [128, 2, 8], F32)
                csw = sb("csw", [128, 2, 8], F32)
                t1 = sb("t1", [128, 2, 256], F32)
                t2 = sb("t2", [128, 2, 256], F32)
                sc = ps("sc", [128, 4, 512], F32)
                junk = ps("junkp", [128, 512], F32)
                oc_ = ps("oc", [128, 512], F32)
                osel_ = ps("osel", [128, 512], F32)
                owin_ = ps("owin", [128, 512], F32)
                oc = oc_[:, 0:388].rearrange("p (r c) -> p r c", c=97)
                selT_ = oc_[:, 400:464].bitcast(BF16)
                selT = selT_[0:96, :]
                osel = osel_[:, 0:260].rearrange("p (r c) -> p r c", c=65)
                owin = owin_[:, 0:260].rearrange("p (r c) -> p r c", c=65)

                S.dma(lambda e: e.dma_start(out=cmpb[0:127, :], in_=dr["cmpb"]), w=["cmpb"])
                S.dma(lambda e: e.dma_start(out=cb[:], in_=dr["cb"]), w=["cb"])
                S.dma(lambda e: e.dma_start(out=acb[:], in_=dr["acb"]), w=["acb"])
                S.dma(lambda e: e.dma_start(out=valid[:], in_=dr["valid"]), w=["valid"])
                S.dma(lambda e: e.dma_start(out=fv[:], in_=dr["fv"]), w=["fv"])
                S.pool(lambda e: e.memset(selpad[:], 0.0), w=[("selpad", 0), ("selpad", 1)])

                iters = [(g, qt) for qt in range(NT) for g in range(2)]
                if STG < 20:
                    iters = iters[:STG]
                cnt = {"sc": 0, "pt": 0}

                def GLv(qt, g, br):
                    return GL[:, qt, :].rearrange("p (h b) -> p h b", b=3)[:, g * 4:(g + 1) * 4, br]

                def qsl(qt):
                    return slice(qt * 128, (qt + 1) * 128)

                def A1(n):
                    g, qt = iters[n]
                    bk = cnt["sc"] % 4
                    cnt["sc"] += 1
                    pbf = cnt["pt"] % 4
                    cnt["pt"] += 1
                    S.pe(lambda e: e.matmul(sc[0:127, bk, :], lhsT=KcT[:, g, 0:127], rhs=QU[:, g, :, qsl(qt)],
                                            start=True, stop=False),
                         r=[("KcT", g), ("QU", g, qt)], w=[("PS", "sc", bk)])
                    S.pe(lambda e: e.matmul(sc[0:127, bk, :], lhsT=idt[0:127, 0:127],
                                            rhs=bc(cmpb[0:127, qsl(qt)], [127, 4, 128], 1), start=False, stop=True),
                         r=["idt", "cmpb"], w=[("PS", "sc", bk)])
                    S.act(lambda e: e.activation(out=PT[0:127, pbf, :], in_=sc[0:127, bk, :], func=AF.Exp, scale=0.125),
                          r=[("PS", "sc", bk)], w=[("PT", pbf)])
                    return pbf

                def A3(n, pbf):
                    g, qt = iters[n]
                    ib = n % 2
                    for r in range(4):
                        S.pe(lambda e, r=r: e.matmul(oc[:, r, :], lhsT=PT[0:127, pbf, r * 128:(r + 1) * 128], rhs=VcM[0:127, g, :],
                                                     start=True, stop=True, skip_group_check=True),
                             r=[("PT", pbf), ("VcMv", g), ("VcMm", g), "VcMone"], w=[("PS", "oc")])
                    S.dve(lambda e: e.tensor_scalar(out=dn[:, ib, :], in0=oc[:, :, 64], scalar1=1e-30, scalar2=None, op0=ALU.max),
                          r=[("PS", "oc")], w=[("dn", ib)])
                    S.dve(lambda e: e.reciprocal(out=rden[:, ib, :], in_=dn[:, ib, :]), r=[("dn", ib)], w=[("rden", ib)])
                    S.dve(lambda e: e.tensor_tensor(out=coefc[:, ib, :], in0=rden[:, ib, :], in1=GLv(qt, g, 0), op=ALU.mult),
                          r=[("rden", ib), ("GL", qt)], w=[("coefc", ib)])
                    S.dve(lambda e: e.tensor_tensor(out=ocg[:, ib, :].rearrange("p (r d) -> p r d", d=64), in0=oc[:, :, 0:64],
                                                    in1=bc(coefc[:, ib, :], [128, 4, 64], 2), op=ALU.mult),
                          r=[("PS", "oc"), ("coefc", ib)], w=[("ocg", ib)])
                    S.dve(lambda e: e.tensor_scalar(out=impa[:, ib, :], in0=oc[:, 0, 65:97], scalar1=rden[:, ib, 0:1], scalar2=None,
                                                    op0=ALU.mult),
                          r=[("PS", "oc"), ("rden", ib)], w=[("impa", ib)])
                    for r in range(1, 4):
                        S.dve(lambda e, r=r: e.scalar_tensor_tensor(out=impa[:, ib, :], in0=oc[:, r, 65:97], scalar=rden[:, ib, r:r + 1],
                                                                    in1=impa[:, ib, :], op0=ALU.mult, op1=ALU.add),
                              r=[("PS", "oc"), ("rden", ib), ("impa", ib)], w=[("impa", ib)])
                    S.dve(lambda e: e.tensor_tensor(out=imp2[:, ib, :], in0=impa[:, ib, :], in1=valid[:, qt, :], op=ALU.mult),
                           r=[("impa", ib), "valid"], w=[("imp2", ib)])
                    S.dve(lambda e: e.tensor_tensor(out=imp2[:, ib, :], in0=imp2[:, ib, :], in1=fv[:, qt, :], op=ALU.add),
                           r=[("imp2", ib), "fv"], w=[("imp2", ib)])
                    S.dve(lambda e: e.max(out=m8[:, ib, :], in_=imp2[:, ib, :]), r=[("imp2", ib)], w=[("m8", ib)])
                    S.dve(lambda e: e.tensor_scalar(out=selpad[:, ib, 64:96], in0=imp2[:, ib, :], scalar1=m8[:, ib, 7:8], scalar2=1.0,
                                                    op0=ALU.is_ge, op1=ALU.subtract),
                          r=[("imp2", ib), ("m8", ib)], w=[("selpad", ib)])

                def A5(n):
                    g, qt = iters[n]
                    ib = n % 2
                    S.pe(lambda e: e.transpose(out=selT, in_=selpad[:, ib, :], identity=idt[:]),
                         r=[("selpad", ib), "idt"], w=[("PS", "oc")])
                    S.act(lambda e: e.activation(out=QG[64:96, g, :, qsl(qt)], in_=bc(selT_[64:96, :], [32, 4, 128], 1),
                                                 func=AF.Copy, scale=BIGSEL),
                          r=[("PS", "oc")], w=[("QGm", g, qt)])

                def tiles_of(n):
                    g, qt = iters[n]
                    L = []
                    for kt in range(0, qt + 1):
                        L.append(("s", kt))
                    for kt in range(max(0, qt - 4), qt + 1):
                        L.append(("w", kt))
                    return L

                def QK(n, tl):
                    g, qt = iters[n]
                    br, kt = tl
                    bk = cnt["sc"] % 4
                    cnt["sc"] += 1
                    ksl = slice(kt * 128, (kt + 1) * 128)
                    if br == "s":
                        bias = cb if kt == qt else None
                        S.pe(lambda e: e.matmul(sc[:, bk, :], lhsT=KE[0:96, g, ksl], rhs=QG[0:96, g, :, qsl(qt)],
                                                start=True, stop=(bias is None)),
                             r=[("KEk", kt), ("KEm", g), ("QGq", g, qt), ("QGm", g, qt)], w=[("PS", "sc", bk)])
                    else:
                        bias = cb if kt == qt else (acb if kt == qt - 4 else None)
                        S.pe(lambda e: e.matmul(sc[:, bk, :], lhsT=KW[0:64, g, ksl], rhs=QG[0:64, g, :, qsl(qt)],
                                                start=True, stop=(bias is None)),
                             r=[("KWk", kt), ("QGq", g, qt)], w=[("PS", "sc", bk)])
                    if bias is not None:
                        S.pe(lambda e: e.matmul(sc[:, bk, :], lhsT=idt[:], rhs=bc(bias[:], [128, 4, 128], 1), start=False, stop=True),
                             r=["idt", "cb", "acb"], w=[("PS", "sc", bk)])
                    return bk

                def EXPPV(n, tl, bk, first_s, first_w, last):
                    g, qt = iters[n]
                    br, kt = tl
                    pbf = cnt["pt"] % 4
                    cnt["pt"] += 1
                    S.act(lambda e: e.activation(out=PT[:, pbf, :], in_=sc[:, bk, :], func=AF.Exp, scale=0.125),
                          r=[("PS", "sc", bk)], w=[("PT", pbf)])
                    acc = osel if br == "s" else owin
                    akey = ("PS", "osel") if br == "s" else ("PS", "owin")
                    vi = g if br == "s" else 2 + g
                    first = first_s if br == "s" else first_w
                    for r in range(4):
                        S.pe(lambda e, r=r: e.matmul(acc[:, r, :], lhsT=PT[:, pbf, r * 128:(r + 1) * 128], rhs=VP[:, kt, vi, :],
                                                     start=(first and r == 0), stop=last, skip_group_check=True),
                             r=[("PT", pbf), ("VP", kt), "VPone"], w=[akey])
                    for _ in range(NDUM):
                        S.pe(I_mm(junk[:, 0:DUMN], idt[:], PT[:, pbf, 0:DUMN], True, True), r=[("PT", pbf)], w=[])

                def FIN(n):
                    g, qt = iters[n]
                    ib = n % 2
                    S.dve(lambda e: e.tensor_copy(out=dsw[:, ib, 0:4], in_=osel[:, :, 64]), r=[("PS", "osel")], w=[("dsw", ib)])
                    S.dve(lambda e: e.tensor_copy(out=dsw[:, ib, 4:8], in_=owin[:, :, 64]), r=[("PS", "owin")], w=[("dsw", ib)])
                    S.dve(lambda e: e.reciprocal(out=rsw[:, ib, :], in_=dsw[:, ib, :]), r=[("dsw", ib)], w=[("rsw", ib)])
                    S.dve(lambda e: e.tensor_tensor(out=csw[:, ib, 0:4], in0=rsw[:, ib, 0:4], in1=GLv(qt, g, 1), op=ALU.mult),
                          r=[("rsw", ib), ("GL", qt)], w=[("csw", ib)])
                    S.dve(lambda e: e.tensor_tensor(out=csw[:, ib, 4:8], in0=rsw[:, ib, 4:8], in1=GLv(qt, g, 2), op=ALU.mult),
                          r=[("rsw", ib), ("GL", qt)], w=[("csw", ib)])
                    t1v = t1[:, ib, :].rearrange("p (r d) -> p r d", d=64)
                    t2v = t2[:, ib, :].rearrange("p (r d) -> p r d", d=64)
                    S.dve(lambda e: e.tensor_tensor(out=t1v, in0=osel[:, :, 0:64], in1=bc(csw[:, ib, 0:4], [128, 4, 64], 2), op=ALU.mult),
                          r=[("PS", "osel"), ("csw", ib)], w=[("t1", ib)])
                    S.dve(lambda e: e.tensor_tensor(out=t2v, in0=owin[:, :, 0:64], in1=bc(csw[:, ib, 4:8], [128, 4, 64], 2), op=ALU.mult),
                          r=[("PS", "owin"), ("csw", ib)], w=[("t2", ib)])
                    S.dve(lambda e: e.tensor_tensor(out=t1[:, ib, :], in0=t1[:, ib, :], in1=ocg[:, ib, :], op=ALU.add),
                           r=[("t1", ib), ("ocg", ib)], w=[("t1", ib)])
                    S.dve(lambda e: e.tensor_tensor(out=t1[:, ib, :], in0=t1[:, ib, :], in1=t2[:, ib, :], op=ALU.add),
                           r=[("t1", ib), ("t2", ib)], w=[("t1", ib)])
                    msl = mix[:, qt, g * 256:(g + 1) * 256]
                    S.dve(lambda e: e.tensor_tensor(out=msl, in0=t1[:, ib, :], in1=msl, op=ALU.mult),
                           r=[("t1", ib), ("mixn", qt)], w=[("mixo", g, qt)])

                if iters:
                    pbf0 = A1(0)
                    A3(0, pbf0)
                    A5(0)
                for n in range(len(iters)):
                    L = tiles_of(n)
                    nxt = n + 1 < len(iters)
                    if nxt:
                        pbn = A1(n + 1)
                    bks = {}
                    LA = int(os.environ.get('KLA', '3'))
                    for ti in range(min(LA, len(L))):
                        bks[ti] = QK(n, L[ti])
                    seen = set()
                    for ti in range(len(L)):
                        if ti + LA < len(L):
                            bks[ti + LA] = QK(n, L[ti + LA])
                        if nxt and ti == 0:
                            A3(n + 1, pbn)
                        br, kt = L[ti]
                        first = br not in seen
                        seen.add(br)
                        qt = iters[n][1]
                        last = (kt == qt)
                        EXPPV(n, L[ti], bks[ti], first, first, last)
                    FIN(n)
                    if nxt:
                        A5(n + 1)
                mo = [("mixo", g, qt) for (g, qt) in iters]
                dump(S, "mix2", mix[:], [128, NT, D], BF16, mo)
                S.emit(nc, esems, dsems, swsems)
                print("phase2 ops", len(S.ops))
            if upto <= 3:
                return nc, dbg_out

        p3 = ExitStack()
        with p3:
            sb, ps, sems = mk(p3)
            S = Sched()
            esems, dsems, swsems = sems("p3", 3)
            wrw = sb("wrw", [128, 8, 1664], BF16)
            wupb = sb("wupb", [128, 512], BF16)
            mu = sb("mu_s", [128, 13], F32)
            mum = sb("mum_s", [128, 13], F32)
            prm = sb("prm", [128, 8, 4], F32)
            gnw = sb("gnw_s", [128, 512], F32)
            gnb = sb("gnb_s", [128, 512], F32)
            rmask = sb("rmask_s", [128, 512], BF16)
            onesbd = sb("onesbd_s", [128, 128], F32)
            ind2 = sb("ind2_s", [128, 2], BF16)
            msu = sb("msu", [128, 128], BF16)
            mui = sb("mui", [128, 128], BF16)
            msl = sb("msl", [128, 128], BF16)
            CAR = sb("CAR", [128, 5, 3], F32)
            TA = sb("TA", [128, 4, 512], BF16)
            PW = sb("PW", [128, 520], F32)
            DW = sb("DW", [128, 512], F32)
            P3 = sb("P3", [128, 2, 3, 520], F32)
            D3 = sb("D3", [128, 2, 2, 512], F32)
            XS = sb("XS", [128, 2, 2, 512], F32)
            E4 = sb("E4", [128, 2, 4, 512], BF16)
            OP = sb("OP", [128, 2, 8, 512], BF16)
            TM = sb("TM", [128, 2, 4, 4, 128], BF16)
            RK = sb("RK", [128, 2, 4, 2], F32)
            GC = sb("GC", [128, 2, 4], F32)
            PA = sb("PA", [128, 2, 2, 2, 4, 128], BF16)
            PTA = sb("PTA", [128, 2, 2, 2, 4, 128], BF16)
            XX = sb("XX", [128, 2, 2, 4, 128], BF16)
            AT = sb("AT", [128, 2, 2, 4, 4, 128], BF16)
            Hf = sb("Hf", [128, 4, 64], F32)
            Hb = sb("Hb", [128, 4, 64], BF16)
            U0b = sb("U0b", [128, 2, 2, 128], BF16)
            Ub = sb("Ub", [128, 2, 2, 128], BF16)
            st1 = sb("st1", [128, 2, 8], F32)
            st2 = sb("st2", [128, 2, 8], F32)
            st3 = sb("st3", [128, 2, 8], F32)
            pb3 = ps("pb3", [128, 8, 512], F32)
            S.dma(I_dma(wrw[:, :, 1536:1664], dr["w_rw"][:, :, 1536:1664]), w=["wrw_wa"], q="pool")
            S.dma(I_dma(wrw[:, :, 0:1536], dr["w_rw"][:, :, 0:1536]), w=["wrw"], q="pool")
            S.dma(I_dma(wupb[:], dr["wup"]), w=["wupb"], q="pool")
            S.dma(I_dma(mu[:], dr["mu13"]), w=["mu"])
            S.dve(I_ts(mum[:], mu[:], -1.0, 1.0, ALU.mult, ALU.add), r=["mu"], w=["mum"])
            for i_, nm in enumerate(("w0c", "a0c", "kkc", "kac", "rkc")):
                S.dma(I_dma(prm[:, i_, :], dr[nm]), w=["prm"])
            S.dve(I_ts(prm[:, 5:7, :], prm[:, 0:2, :], -1.0, None, ALU.mult), r=["prm"], w=["prm2"])
            S.dve(I_ts(prm[:, 7, :], prm[:, 3, :], -1.0, 1.0, ALU.mult, ALU.add), r=["prm"], w=["prm3"])
            for nm, dst in (("gnwbc", gnw), ("gnbbc", gnb), ("rmask", rmask), ("onesbd", onesbd), ("ind2", ind2),
                            ("m_su", msu), ("m_ui", mui), ("m_sl", msl)):
                S.dma(I_dma(dst[:], dr[nm]), w=[nm])
            S.pool(I_ms(CAR[:], 0.0), w=["CAR"])
            S.pool(I_ms(Hf[:], 0.0), w=[("Hf", j_) for j_ in range(4)])
            S.pool(I_ms(Hb[:], 0.0), w=[("Hb", j_) for j_ in range(4)])
            PRM = ["prm", "prm2", "prm3"]
            NTB = 4 if STG >= 40 else (min(4, max(1, STG - 29)) if STG >= 30 else 4)
            NJ = 4

            def blk_head(tb):
                tsl = slice(tb * 512, (tb + 1) * 512)
                tpar = tb
                for k in range(8):
                    S.pe(I_mm(pb3[:, 0, :], wrw[:, k, 1536:1664], xT[:, k, tsl], k == 0, k == 7), r=["wrw_wa"], w=[("PS", "b", 0)])
                S.act(I_act(PW[:, 8:520], pb3[:, 0, :], AF.Copy), r=[("PS", "b", 0)], w=["PW"])
                S.dve(I_cp(PW[:, 7:8], CAR[:, 4, 0:1]), r=["CAR", ("CARw", 4)], w=["PW"])
                S.dve(I_cp(CAR[:, 4, 0:1], PW[:, 519:520]), r=["PW"], w=[("CARw", 4)])
                S.dve(I_tt(DW[:], PW[:, 7:519], PW[:, 8:520], ALU.subtract), r=["PW"], w=["DW"])
                S.dve(I_stt(DW[:], DW[:], mu[:, 12:13], PW[:, 8:520], ALU.mult, ALU.add), r=["DW", "PW", "mu"], w=["DW"])
                S.act(I_act(PW[0:64, 8:520], DW[0:64, :], AF.Exp, scale=2.0), r=["DW"], w=["PW"])
                S.dve(I_ts(PW[0:64, 8:520], PW[0:64, 8:520], 1.0, None, ALU.add), r=["PW"], w=["PW"])
                S.dve(I_rcp(PW[0:64, 8:520], PW[0:64, 8:520]), r=["PW"], w=["PW"])
                S.dve(I_ts(TA[0:64, tpar, :], PW[0:64, 8:520], -2.0, 1.0, ALU.mult, ALU.add), r=["PW"], w=[("TA", tpar)])
                S.act(I_act(TA[64:128, tpar, :], DW[64:128, :], AF.Copy), r=["DW"], w=[("TA", tpar)])

            def pair(tb, j, S):
                tsl = slice(tb * 512, (tb + 1) * 512)
                tpar = tb
                par = (tb * 4 + j) % 2
                BG, BT, BN0, BN1 = 4 * par, 4 * par + 1, 4 * par + 2, 4 * par + 3
                ptv = pb3[:, BT, :].bitcast(BF16).rearrange("p (a k t) -> p a k t", a=2, k=4, t=128)
                kP = [("P3", par, i_) for i_ in range(3)]
                kD = [("D3", par, 0), ("D3", par, 1)]
                kX = [("XS", par, 0), ("XS", par, 1)]
                kOP = [("OP", par, i_) for i_ in range(8)]
                kE = [("E4", par, i_) for i_ in range(4)]
                for i_ in range(3):
                    c0 = i_ * 512 + j * 128
                    bk = (BG, BN0, BN1)[i_]
                    for k in range(8):
                        S.pe(I_mm(pb3[:, bk, :], wrw[:, k, c0:c0 + 128], xT[:, k, tsl], k == 0, k == 7), r=["wrw"], w=[("PS", "b", bk)])
                    if False:
                        pass
                    else:
                        S.act(I_act(P3[:, par, i_, 8:520], pb3[:, bk, :], AF.Copy), r=[("PS", "b", bk)], w=[kP[i_]])
                S.act(I_act(P3[:, par, :, 7:8], CAR[:, j, :].unsqueeze(2), AF.Copy), r=["CAR", ("CARw", j)], w=kP)
                S.act(I_act(CAR[:, j, :].unsqueeze(2), P3[:, par, :, 519:520], AF.Copy), r=kP, w=[("CARw", j)])

                def shift(i_, out_ap, okeys):
                    dd = D3[:, par, i_ % 2, :]
                    S.act(I_act(dd, P3[:, par, i_, 8:520], AF.Copy, scale=mum[:, i_ * 4 + j:i_ * 4 + j + 1]), r=[kP[i_], "mum"], w=[kD[i_ % 2]])
                    S.dve(I_stt(out_ap, P3[:, par, i_, 7:519], mu[:, i_ * 4 + j:i_ * 4 + j + 1], dd, ALU.mult, ALU.add),
                          r=[kD[i_ % 2], kP[i_], "mu"], w=okeys)

                shift(0, XS[:, par, 0, :], [kX[0]])
                shift(1, XS[:, par, 1, :], [kX[1]])
                shift(2, OP[:, par, 7, :], [kOP[7]])
                S0 = P3[:, par, 0, 8:520]
                S1 = P3[:, par, 1, 8:520]
                S2 = P3[:, par, 2, 8:520]
                S3 = D3[:, par, 0, :]
                S4 = D3[:, par, 1, :]
                XR = XS[:, par, 0, :]
                XK = XS[:, par, 1, :]
                k0, k1, k2, k3, k4 = kP[0], kP[1], kP[2], kD[0], kD[1]
                bkw = BT
                S.pe(I_mm(pb3[:, bkw, :], wupb[0:64, j * 128:(j + 1) * 128], TA[0:64, tpar, :]), r=["wupb", ("TA", tpar)], w=[("PS", "b", bkw)])
                S.act(I_act(S0, pb3[:, bkw, :], AF.Exp, scale=-1.0, bias=prm[:, 5, j:j + 1]), r=[("PS", "b", bkw)] + PRM + [kX[0]], w=[k0])
                S.act(I_act(S0, S0, AF.Ln, bias=1.0), r=[k0], w=[k0])
                S.act(I_act(S0, S0, AF.Exp, scale=-1.0, bias=-0.5), r=[k0], w=[k0])
                bka = BG
                S.pe(I_mm(pb3[:, bka, :], wupb[64:128, j * 128:(j + 1) * 128], TA[64:128, tpar, :]), r=["wupb", ("TA", tpar)], w=[("PS", "b", bka)])
                S.act(I_act(S1, pb3[:, bka, :], AF.Exp, scale=-1.0, bias=prm[:, 6, j:j + 1]), r=[("PS", "b", bka)] + PRM + [kX[1]], w=[k1])
                S.act(I_act(S1, S1, AF.Ln, bias=1.0), r=[k1], w=[k1])
                S.act(I_act(S1, S1, AF.Exp, scale=-1.0), r=[k1], w=[k1])
                S.act(I_act(S2, XK, AF.Copy, scale=prm[:, 2, j:j + 1]), r=[kX[1], kOP[7]] + PRM, w=[k2])
                S.act(I_act(S3, S2, AF.Square), r=[k2, kX[0], kX[1], kOP[7]], w=[k3])
                bkn = BN0
                S.pe(I_mm(pb3[:, bkn, :], onesbd[:], S3), r=["onesbd", k3], w=[("PS", "b", bkn)])
                S.act(I_act(S3, pb3[:, bkn, :], AF.Ln, bias=1e-24), r=[("PS", "b", bkn)], w=[k3])
                S.act(I_act(S3, S3, AF.Exp, scale=-0.5), r=[k3], w=[k3])
                S.dve(I_tt(S2, S2, S3, ALU.mult), r=[k2, k3], w=[k2])
                S.act(I_act(S3, S1, AF.Identity, scale=prm[:, 3, j:j + 1], bias=prm[:, 7, j:j + 1]), r=[k1, k2] + PRM, w=[k3])
                S.dve(I_tt(XK, XK, S3, ALU.mult), r=[kX[1], k3, k2], w=[kX[1]])
                S.dve(I_tt(S1, S2, S1, ALU.mult), r=[k2, k1, k3], w=[k1])
                S.dve(lambda e: e.tensor_tensor_scan(out=S3, data0=rmask[:], data1=S0, initial=0.0, op0=ALU.mult, op1=ALU.subtract),
                      r=["rmask", k0, kX[1], k1], w=[k3])
                S.dve(I_tt(S4, S3, S0, ALU.add), r=[k3, k0], w=[k4])
                S.act(I_act(E4[:, par, 0, :], S4, AF.Exp), r=[k4], w=[kE[0]])
                S.act(I_act(E4[:, par, 1, :], S3, AF.Exp), r=[k3], w=[kE[1]])
                S.act(I_act(E4[:, par, 2, :], S3, AF.Exp, scale=-1.0), r=[k3], w=[kE[2]])
                c3 = S3.rearrange("p (c t) -> p c t", t=128)
                S.act(I_act(GC[:, par, :], c3[:, :, 127], AF.Exp), r=[k3], w=[("GC", par)])
                S.dve(I_tt(S4.rearrange("p (c t) -> p c t", t=128), bc(c3[:, :, 127], [128, 4, 128], 2), c3, ALU.subtract), r=[k3, kE[0]], w=[k4])
                S.act(I_act(E4[:, par, 3, :], S4, AF.Exp), r=[k4], w=[kE[3]])
                S.dve(I_stt(OP[:, par, 0, :], S2, -1.0, E4[:, par, 0, :], ALU.mult, ALU.mult), r=[k2, kE[0]], w=[kOP[0]])
                S.dve(I_tt(OP[:, par, 1, :], XR, E4[:, par, 1, :], ALU.mult), r=[kX[0], kE[1]], w=[kOP[1]])
                S.dve(I_tt(OP[:, par, 2, :], S1, E4[:, par, 2, :], ALU.mult), r=[k1, kE[2]], w=[kOP[2]])
                S.dve(I_tt(OP[:, par, 3, :], XK, E4[:, par, 2, :], ALU.mult), r=[kX[1], kE[2]], w=[kOP[3]])
                S.dve(I_tt(OP[:, par, 4, :], S1, E4[:, par, 3, :], ALU.mult), r=[k1, kE[3]], w=[kOP[4]])
                S.dve(I_tt(OP[:, par, 5, :], XK, E4[:, par, 3, :], ALU.mult), r=[kX[1], kE[3]], w=[kOP[5]])
                S.dve(I_stt(OP[:, par, 6, :], XR, prm[:, 4, j:j + 1], XK, ALU.mult, ALU.mult), r=[kX[0], kX[1]] + PRM, w=[kOP[6]])
                for c in range(4):
                    csl = slice(c * 128, (c + 1) * 128)
                    hb_ = c % 2
                    for ki_, kind in enumerate((4, 5, 7)):
                        S.pe(I_tr(ptv[:, hb_, ki_, :], OP[:, par, kind, csl], idt[:]), r=[kOP[kind], "idt"], w=[("PS", "b", BT)])
                    S.act(I_act(TM[:, par, c, 0:3, :], ptv[:, hb_, 0:3, :], AF.Copy), r=[("PS", "b", BT)], w=[("TM", par, c)])
                    bkr = BG
                    S.pe(I_mm(pb3[:, bkr, 0:2], OP[:, par, 6, csl], ind2[:]), r=[kOP[6], "ind2"], w=[("PS", "b", bkr)])
                    S.act(I_act(RK[:, par, c, :], pb3[:, bkr, 0:2], AF.Copy), r=[("PS", "b", bkr)], w=[("RK", par)])
                S_pair = S
                heads = []
                for h in range(2):
                    S = Sched()
                    heads.append(S)
                    hs = slice(h * 64, (h + 1) * 64)
                    kAT = ("AT", par, h)
                    hbanks = ((BN0, BG), (BN1, BT))[h]
                    hcnt = [0]

                    def nxt(_pool, hbanks=hbanks, hcnt=hcnt):
                        b_ = hbanks[hcnt[0] % 2]
                        hcnt[0] += 1
                        return b_
                    bL = hbanks[1]
                    for c in range(4):
                        csl = slice(c * 128, (c + 1) * 128)
                        bA = hbanks[0]
                        gA = pb3[:, bA, :].rearrange("p (k t) -> p k t", t=128)
                        kA = ("PS", "b", bA)
                        S.pe(I_mm(gA[:, 0, :], OP[hs, par, 2, csl], OP[hs, par, 0, csl], True, True), r=[kOP[2], kOP[0]], w=[kA])
                        S.pe(I_mm(gA[:, 1, :], OP[hs, par, 3, csl], OP[hs, par, 0, csl], False, True), r=[kOP[3], kOP[0]], w=[kA])
                        S.pe(I_mm(gA[:, 2, :], OP[hs, par, 2, csl], OP[hs, par, 1, csl], False, True), r=[kOP[2], kOP[1]], w=[kA])
                        S.pe(I_mm(gA[:, 3, :], OP[hs, par, 3, csl], OP[hs, par, 1, csl], False, True), r=[kOP[3], kOP[1]], w=[kA])
                        S.dve(I_tt(AT[:, par, h, c, 0:2, :], gA[:, 0:2, :], bc(msu[:], [128, 2, 128], 1), ALU.mult), r=[kA, "m_su"], w=[kAT])
                        S.dve(I_tt(AT[:, par, h, c, 2:4, :], gA[:, 2:4, :], bc(mui[:], [128, 2, 128], 1), ALU.mult), r=[kA, "m_ui"], w=[kAT])
                        S.pe(I_mm(pb3[:, bL, csl], OP[hs, par, 0, csl], OP[hs, par, 2, csl], c == 0, True), r=[kOP[0], kOP[2]], w=[("PS", "b", bL)])
                    S.dve(I_tt(PA[:, par, h, 0, :, :], pb3[:, bL, :].rearrange("p (c t) -> p c t", t=128), bc(msl[:], [128, 4, 128], 1), ALU.mult),
                          r=[("PS", "b", bL), "m_sl"], w=[("PA", par, h, 0)])
                    S.dve(I_tt(XX[:, par, h, :, :], AT[:, par, h, :, 0, :], bc(idt[:], [128, 4, 128], 1), ALU.add), r=[kAT, "idt"], w=[("XX", par, h)])
                    for k in range(1, 7):
                        pi, po = (k - 1) % 2, k % 2
                        bP = nxt("nb")
                        for c in range(4):
                            csl = slice(c * 128, (c + 1) * 128)
                            ptk = (AT[:, par, h, c, 0, :] if k == 1 else PTA[:, par, h, pi, c, :])
                            S.pe(I_mm(pb3[:, bP, csl], ptk, PA[:, par, h, pi, c, :], c == 0, True), r=[kAT, ("PTA", par, h, pi), ("PA", par, h, pi)], w=[("PS", "b", bP)])
                        S.act(I_act(PA[:, par, h, po, :, :], pb3[:, bP, :].rearrange("p (c t) -> p c t", t=128), AF.Copy), r=[("PS", "b", bP)], w=[("PA", par, h, po)])
                        if k < 6:
                            bT = nxt("nb")
                            for c in range(4):
                                csl = slice(c * 128, (c + 1) * 128)
                                ptk = (AT[:, par, h, c, 0, :] if k == 1 else PTA[:, par, h, pi, c, :])
                                S.pe(I_mm(pb3[:, bT, csl], PA[:, par, h, pi, c, :], ptk, c == 0, True), r=[kAT, ("PTA", par, h, pi), ("PA", par, h, pi)], w=[("PS", "b", bT)])
                            if k % 2 == 1:
                                S.dve(I_cp(PTA[:, par, h, po, :, :], pb3[:, bT, :].rearrange("p (c t) -> p c t", t=128)), r=[("PS", "b", bT)], w=[("PTA", par, h, po)])
                            else:
                                S.act(I_act(PTA[:, par, h, po, :, :], pb3[:, bT, :].rearrange("p (c t) -> p c t", t=128), AF.Copy), r=[("PS", "b", bT)], w=[("PTA", par, h, po)])
                        bX = nxt("nb")
                        for _ in range(NDUM3):
                            S.pe(I_mm(pb3[:, bX, :], idt[:], OP[:, par, 0, :], True, True), r=[kOP[0]], w=[("PS", "b", bX)])
                        for c in range(4):
                            csl = slice(c * 128, (c + 1) * 128)
                            S.pe(I_mm(pb3[:, bX, csl], PA[:, par, h, po, c, :], XX[:, par, h, c, :], c == 0, True), r=[("PA", par, h, po), ("XX", par, h)], w=[("PS", "b", bX)])
                        S.dve(I_tt(XX[:, par, h, :, :], pb3[:, bX, :].rearrange("p (c t) -> p c t", t=128), XX[:, par, h, :, :], ALU.add),
                              r=[("PS", "b", bX), ("XX", par, h)], w=[("XX", par, h)])
                S = S_pair
                S.ops.extend(merge2(heads[0].ops, heads[1].ops))
                for c in range(4):
                    csl = slice(c * 128, (c + 1) * 128)
                    ub = c % 2
                    b0 = BG
                    k0_ = ("PS", "b", b0)
                    for h in range(2):
                        hs = slice(h * 64, (h + 1) * 64)
                        S.pe(I_mm(pb3[:, b0, hs], OP[hs, par, 0, csl], Hb[hs, j, :], h == 0, False), r=[kOP[0], ("Hb", j)], w=[k0_])
                        S.pe(I_mm(pb3[:, b0, hs], AT[:, par, h, c, 1, :], TM[:, par, c, 2, hs], False, True), r=[("AT", par, h), ("TM", par, c)], w=[k0_])
                    S.act(I_act(U0b[:, par, ub, :], pb3[:, b0, 0:128], AF.Copy), r=[k0_], w=[("U0b", par, ub)])
                    b1 = BT
                    k1_ = ("PS", "b", b1)
                    for h in range(2):
                        hs = slice(h * 64, (h + 1) * 64)
                        S.pe(I_mm(pb3[:, b1, hs], XX[:, par, h, c, :], U0b[:, par, ub, hs], h == 0, True), r=[("XX", par, h), ("U0b", par, ub)], w=[k1_])
                    S.act(I_act(Ub[:, par, ub, :], pb3[:, b1, 0:128], AF.Copy), r=[k1_], w=[("Ub", par, ub)])
                    b3 = BN1
                    k3_ = ("PS", "b", b3)
                    for h in range(2):
                        hs = slice(h * 64, (h + 1) * 64)
                        S.pe(I_mm(pb3[hs, b3, 0:64], TM[:, par, c, 1, hs], TM[:, par, c, 2, hs], True, False), r=[("TM", par, c)], w=[k3_])
                    for h in range(2):
                        hs = slice(h * 64, (h + 1) * 64)
                        S.pe(I_mm(pb3[hs, b3, 0:64], TM[:, par, c, 0, hs], Ub[:, par, ub, hs], False, True), r=[("TM", par, c), ("Ub", par, ub)], w=[k3_])
                    b2 = BN0
                    k2_ = ("PS", "b", b2)
                    for h in range(2):
                        hs = slice(h * 64, (h + 1) * 64)
                        S.pe(I_mm(pb3[:, b2, hs], OP[hs, par, 1, csl], Hb[hs, j, :], h == 0, False), r=[kOP[1], ("Hb", j)], w=[k2_])
                        S.pe(I_mm(pb3[:, b2, hs], AT[:, par, h, c, 3, :], TM[:, par, c, 2, hs], False, False), r=[("AT", par, h), ("TM", par, c)], w=[k2_])
                        S.pe(I_mm(pb3[:, b2, hs], AT[:, par, h, c, 2, :], Ub[:, par, ub, hs], False, True), r=[("AT", par, h), ("Ub", par, ub)], w=[k2_])
                    S.act(I_act(P3[:, par, 0, 8 + c * 128:8 + (c + 1) * 128], pb3[:, b2, 0:128], AF.Copy), r=[k2_], w=[kP[0]])
                    S.dve(I_stt(Hf[:, j, :], Hf[:, j, :], GC[:, par, c:c + 1], pb3[:, b3, 0:64], ALU.mult, ALU.add),
                          r=[k3_, ("Hf", j), ("GC", par)], w=[("Hf", j)])
                    S.act(I_act(Hb[:, j, :], Hf[:, j, :], AF.Copy), r=[("Hf", j)], w=[("Hb", j)])
                Y3 = P3[:, par, 0, 8:520].rearrange("p (a d) -> p a d", d=64)
                G1v = D3[:, par, 0, :]
                G2v = D3[:, par, 1, :]
                g1 = G1v.rearrange("p (a d) -> p a d", d=64)
                kY = kP[0]
                kG1, kG2 = kD[0], kD[1]
                s1, s2, s3 = st1[:, par, :], st2[:, par, :], st3[:, par, :]
                ks1, ks2, ks3 = ("st1", par), ("st2", par), ("st3", par)
                S.dve(lambda e: e.tensor_reduce(out=s1, in_=Y3, axis=AX.X, op=ALU.add), r=[kY], w=[ks1])
                S.act(I_act(g1, Y3, AF.Square), r=[kY], w=[kG1])
                S.dve(lambda e: e.tensor_reduce(out=s2, in_=g1, axis=AX.X, op=ALU.add), r=[kG1], w=[ks2])
                S.dve(I_ts(s1, s1, 1.0 / 64, None, ALU.mult), r=[ks1], w=[ks1])
                S.dve(I_tt(s3, s1, s1, ALU.mult), r=[ks1], w=[ks3])
                S.dve(I_stt(s2, s2, 1.0 / 64, s3, ALU.mult, ALU.subtract), r=[ks2, ks3], w=[ks2])
                S.act(I_act(s2, s2, AF.Ln, bias=64e-5), r=[ks2], w=[ks2])
                S.act(I_act(s2, s2, AF.Exp, scale=-0.5), r=[ks2], w=[ks2])
                S.dve(I_tt(g1, Y3, bc(s1, [128, 8, 64], 2), ALU.subtract), r=[kY, ks1, kG1], w=[kG1])
                S.dve(I_tt(g1, g1, bc(s2, [128, 8, 64], 2), ALU.mult), r=[kG1, ks2], w=[kG1])
                g1c = G1v.rearrange("p (c f) -> p c f", f=128)
                S.dve(I_tt(g1c, g1c, bc(gnw[:, j * 128:(j + 1) * 128], [128, 4, 128], 1), ALU.mult), r=[kG1, "gnwbc"], w=[kG1])
                S.dve(I_tt(g1c, g1c, bc(gnb[:, j * 128:(j + 1) * 128], [128, 4, 128], 1), ALU.add), r=[kG1, "gnbbc"], w=[kG1])
                V3 = TM[:, par, :, 2, :].rearrange("p c (h d) -> p c h d", d=64)
                S.dve(I_tt(G2v.rearrange("p (c h d) -> p c h d", h=2, d=64), V3, bc(RK[:, par, :, :], [128, 4, 2, 64], 3), ALU.mult),
                      r=[("TM", par, 0), ("TM", par, 1), ("TM", par, 2), ("TM", par, 3), ("RK", par)], w=[kG2])
                S.dve(I_tt(G1v, G1v, G2v, ALU.add), r=[kG1, kG2], w=[kG1])
                msl_ = mix[:, tb * 4:(tb + 1) * 4, 512 + j * 128:512 + (j + 1) * 128]
                S.dve(I_tt(msl_, g1c, msl_, ALU.mult), r=[kG1], w=[("mixr3", tb, j)])

            for tb in range(NTB):
                blk_head(tb)
            seqs = [[], []]
            allops = []
            for tb in range(NTB):
                for j in range(NJ):
                    Sp = Sched()
                    pair(tb, j, Sp)
                    seqs[(tb * 4 + j) % 2].extend(Sp.ops)
                    allops.extend(Sp.ops)
            if os.environ.get("KMERGE", "1") == "1":
                S.ops.extend(merge2(seqs[0], seqs[1], float(os.environ.get("KOFF", "0"))))
                print("phase3 est makespan", merge2.last_makespan)
            else:
                S.ops.extend(allops)
            dump(S, "mix3", mix[:], [128, NT, D], BF16, [("mixr3", tb_, j_) for tb_ in range(NTB) for j_ in range(NJ)])
            S.emit(nc, esems, dsems, swsems)
            print("phase3 ops", len(S.ops))
        if upto <= 4:
            return nc, dbg_out

        p4 = ExitStack()
        with p4:
            sb, ps, sems = mk(p4)
            S = Sched()
            esems, dsems, swsems = sems("p4", 2)
            wo = sb("wo", [128, 8, D], BF16)
            fg = sb("fg", [128, D], F32)
            NTH = 4
            mT = sb("mT", [128, NTH, 8, 128], BF16)
            xr_ = sb("xr4", [128, NTH, D], F32)
            yb = sb("yb", [128, NTH, D], F32)
            ob = sb("ob", [128, NTH, D], F32)
            sq = sb("sq4", [128, NTH, D], BF16)
            ss4 = sb("ss4", [128, NT], F32)
            ln4 = sb("ln4", [128, NT], F32)
            rs4 = sb("rs4", [128, NT], F32)
            pm = ps("pm", [128, NTH, 8, 128], BF16)
            po = ps("po", [128, NTH, 512], F32)
            for hf in range(2):
                S.dma(I_dma(wo[:, :, hf * 512:(hf + 1) * 512], dr["w_out"][:, :, hf * 512:(hf + 1) * 512]), w=[("wo", hf)], q="pool")
            S.dma(lambda e: e.dma_start(out=fg[:], in_=dr["fgbc"]), w=["fg"])
            S.pool(lambda e: e.memset(ss4[:], 0.0), w=["ss4"])

            def out_thread(tt, S_):
                b = tt % NTH
                S_.dma(I_dma(xr_[:, b, :], x[tt * 128:(tt + 1) * 128, :]), w=[("xr4", b)])
                for k in range(8):
                    S_.pe(I_tr(pm[:, b, k, :], mix[:, tt, k * 128:(k + 1) * 128], idt[:]), r=["idt"], w=[("PS", "pm", b)])
                S_.act(I_act(mT[:, b, :, :], pm[:, b, :, :], AF.Copy), r=[("PS", "pm", b)], w=[("mT", b)])
                for hf in range(2):
                    for k in range(8):
                        S_.pe(I_mm(po[:, b, :], mT[:, b, k, :], wo[:, k, hf * 512:(hf + 1) * 512], k == 0, k == 7),
                              r=[("mT", b), ("wo", hf)], w=[("PS", "po", b)])
                    S_.dve(I_tt(yb[:, b, hf * 512:(hf + 1) * 512], po[:, b, :], xr_[:, b, hf * 512:(hf + 1) * 512], ALU.add),
                           r=[("PS", "po", b), ("xr4", b)], w=[("yb", b, hf)])
                S_.act(I_act(sq[:, b, :], yb[:, b, :], AF.Square, accum_out=ss4[:, tt:tt + 1]),
                       r=[("yb", b, 0), ("yb", b, 1), "ss4"], w=[("sq4", b), ("ss4", tt)])
                S_.act(I_act(ln4[:, tt:tt + 1], ss4[:, tt:tt + 1], AF.Ln, scale=1.0 / D, bias=1e-6), r=[("ss4", tt)], w=[("ln4", tt)])
                S_.act(I_act(rs4[:, tt:tt + 1], ln4[:, tt:tt + 1], AF.Exp, scale=-0.5), r=[("ln4", tt)], w=[("rs4", tt)])
                S_.dve(I_stt(ob[:, b, :], yb[:, b, :], rs4[:, tt:tt + 1], fg[:], ALU.mult, ALU.mult),
                       r=[("yb", b, 0), ("yb", b, 1), ("rs4", tt), "fg"], w=[("ob", b)])
                S_.dma(I_dma(out[tt * 128:(tt + 1) * 128, :], ob[:, b, :]), r=[("ob", b)], w=[("out", tt)])

            seqs = [[] for _ in range(NTH)]
            for tt in range(NT):
                Sp = Sched()
                out_thread(tt, Sp)
                seqs[tt % NTH].extend(Sp.ops)
            S.ops.extend(merge2(merge2(seqs[0], seqs[1]), merge2(seqs[2], seqs[3])))
            S.emit(nc, esems, dsems, swsems)
            print("phase4 ops", len(S.ops))
    return nc, dbg_out


_CACHE = {}


def kernel(**inputs):
    if "nc" not in _CACHE:
        _CACHE["nc"] = build()[0]
        _CACHE["consts"] = host_consts()
    nc = _CACHE["nc"]
    shared = dict(_CACHE["consts"])
    shared.update(host_weights(inputs))
    xin = np.asarray(inputs["x"], dtype=np.float32)
    B = xin.shape[0]
    in_maps = []
    for b in range(B):
        m = dict(shared)
        m["x"] = np.ascontiguousarray(xin[b])
        in_maps.append(m)
    res = run_bass_kernel_spmd(nc, in_maps, core_ids=list(range(B)))
    return np.stack([np.asarray(r["out"], dtype=np.float32) for r in res.results], axis=0)
```
